# Optimizing a Trainium2 kernel written in Bass

```python
import math
import jax, jax.numpy as jnp
from jax import lax
import numpy as np

D_MODEL = 2048
BATCH = 8
SEQ = 4096
DEPTH = 4

A_WINDOWS = (128, 512, 2048)
A_DILATIONS = (1, 4, 16)
A_GROUPS = len(A_WINDOWS)
A_HEADS = 8
HEAD_DIM = 128
A_WIDTH = A_HEADS * HEAD_DIM
ATTN_BLOCK = 128
B_WIDTH = D_MODEL - A_WIDTH
B_CONV = 3
C_WIDTH = D_MODEL // 2
C_CONV = 31
D_WIDTH = D_MODEL - C_WIDTH
D_WINDOWS = (2, 4, 8, 16)
D_GROUP = D_WIDTH // len(D_WINDOWS)
D_FF = ((8 * D_MODEL // 3 + 255) // 256) * 256
EPS = 1e-6
N_EVEN = (DEPTH + 1) // 2
N_ODD = DEPTH // 2
EVEN_IN = 3 * A_GROUPS * A_WIDTH + 3 * B_WIDTH
ODD_IN = 2 * C_WIDTH + D_WIDTH

kernel_name = "hybrid_dilated_attn_shortconv_conformer_pool"


def _rmsnorm(x, g):
    xf = x.astype(jnp.float32)
    y = xf * lax.rsqrt(jnp.mean(xf * xf, axis=-1, keepdims=True) + EPS)
    return (y * g.astype(jnp.float32)).astype(x.dtype)


def _layernorm(x, g, b):
    xf = x.astype(jnp.float32)
    mu = jnp.mean(xf, axis=-1, keepdims=True)
    var = jnp.mean(jnp.square(xf - mu), axis=-1, keepdims=True)
    y = (xf - mu) * lax.rsqrt(var + EPS)
    return (y * g.astype(jnp.float32) + b.astype(jnp.float32)).astype(x.dtype)


def _swiglu(xn, w1, w3, w2):
    return (jax.nn.silu(xn @ w1) * (xn @ w3)) @ w2


def _causal_dwconv(u, w):
    K, C = w.shape
    return lax.conv_general_dilated(u, w[:, None, :].astype(u.dtype), window_strides=(1,),
                                    padding=[(K - 1, 0)],
                                    dimension_numbers=('NWC', 'WIO', 'NWC'),
                                    feature_group_count=C)


def _alibi_slopes(n):
    return jnp.asarray(2.0 ** (-8.0 * np.arange(1, n + 1) / n), dtype=jnp.float32)


def _dilated_attention(q, k, v, dilation, window, slopes):
    Bn, S, H, Dh = q.shape
    d = dilation
    L = S // d
    Q = ATTN_BLOCK
    w_sub = window // d
    nblk = -(-L // Q)
    Lp = nblk * Q

    def blocks(t):
        t = t.reshape(Bn, L, d, H, Dh).transpose(0, 2, 1, 3, 4)
        t = jnp.pad(t, ((0, 0), (0, 0), (0, Lp - L), (0, 0), (0, 0)))
        return t.reshape(Bn, d, nblk, Q, H, Dh)

    def with_prev(t):
        prev = jnp.pad(t, ((0, 0), (0, 0), (1, 0), (0, 0), (0, 0), (0, 0)))[:, :, :-1]
        return jnp.concatenate([prev, t], axis=3)

    qb = blocks(q)
    kk = with_prev(blocks(k))
    vv = with_prev(blocks(v))
    s = jnp.einsum('brnqhd,brnkhd->brnhqk', qb, kk,
                   preferred_element_type=jnp.float32) * (Dh ** -0.5)
    qi = jnp.arange(Q)[:, None]
    ci = jnp.arange(2 * Q)[None, :]
    dist = qi + Q - ci
    key_pos = jnp.arange(nblk)[:, None, None] * Q - Q + ci[None]
    valid = (dist >= 0) & (dist <= w_sub) & (key_pos >= 0)
    bias = -slopes[:, None, None] * (dist * d).astype(jnp.float32)
    s = jnp.where(valid[:, None], s + bias, -jnp.inf)
    lse = jax.nn.logsumexp(s, axis=-1)
    p = jnp.exp(s - lse[..., None])
    o = jnp.einsum('brnhqk,brnkhd->brnqhd', p.astype(v.dtype), vv,
                   preferred_element_type=jnp.float32)

    def unblock(t):
        t = t.reshape((Bn, d, Lp) + t.shape[4:])[:, :, :L]
        t = jnp.swapaxes(t, 1, 2)
        return t.reshape((Bn, S) + t.shape[3:])

    return unblock(o), unblock(jnp.swapaxes(lse, 3, 4))


def _even_mixer(xn, w_in, q_gain, k_gain, conv_w, w_out):
    Bn, S, _ = xn.shape
    h = xn @ w_in
    n_qkv = 3 * A_GROUPS * A_WIDTH
    qkv = h[..., :n_qkv].reshape(Bn, S, 3, A_GROUPS, A_HEADS, HEAD_DIM)
    q = _rmsnorm(qkv[:, :, 0], q_gain)
    k = _rmsnorm(qkv[:, :, 1], k_gain)
    v = qkv[:, :, 2]
    slopes = _alibi_slopes(A_HEADS)
    outs, lses = [], []
    for g in range(A_GROUPS):
        o, l = _dilated_attention(q[:, :, g], k[:, :, g], v[:, :, g],
                                  A_DILATIONS[g], A_WINDOWS[g], slopes)
        outs.append(o)
        lses.append(l)
    alpha = jax.nn.softmax(jnp.stack(lses), axis=0)
    y_a = jnp.einsum('gbsh,gbshd->bshd', alpha, jnp.stack(outs))
    y_a = y_a.reshape(Bn, S, A_WIDTH).astype(xn.dtype)
    b_gate, c_gate, xt = jnp.split(h[..., n_qkv:], 3, axis=-1)
    y_b = b_gate * _causal_dwconv(c_gate * xt, conv_w)
    return jnp.concatenate([y_a, y_b], axis=-1) @ w_out


def _odd_mixer(xn, w_in, conv_w, conv_b, ln_g, ln_b, pool_w, pool_scale, w_out):
    Bn, S, _ = xn.shape
    h = xn @ w_in
    u = h[..., :C_WIDTH] * jax.nn.sigmoid(h[..., C_WIDTH:2 * C_WIDTH])
    u = _causal_dwconv(u, conv_w) + conv_b
    u = jax.nn.silu(_layernorm(u, ln_g, ln_b))
    z = h[..., 2 * C_WIDTH:].reshape(Bn, S, len(D_WINDOWS), D_GROUP)
    zf = z.astype(jnp.float32)
    cs = jnp.cumsum(zf, axis=1)
    t1 = jnp.arange(1, S + 1, dtype=jnp.float32)
    pooled = []
    for g, kw in enumerate(D_WINDOWS):
        c = cs[:, :, g]
        lo = jnp.pad(c, ((0, 0), (kw, 0), (0, 0)))[:, :S]
        pooled.append((c - lo) / jnp.minimum(t1, float(kw))[None, :, None])
    pooled = jnp.stack(pooled, axis=2) - zf
    y_d = jnp.einsum('bsgc,gce->bsge', pooled.astype(xn.dtype), pool_w)
    y_d = y_d.reshape(Bn, S, D_WIDTH) * pool_scale
    return jnp.concatenate([u, y_d], axis=-1) @ w_out


def setup_inputs(seed: int = 0) -> dict:
    key = jax.random.key(seed)
    ks = jax.random.split(key, 20)
    f32 = jnp.float32

    def nrm(k, shape, scale):
        return jax.random.normal(k, shape, f32) * scale

    return {
        "x": nrm(ks[0], (BATCH, SEQ, D_MODEL), 1.0),
        "norm_g": 1.0 + nrm(ks[1], (DEPTH, 3, D_MODEL), 0.02),
        "ffn_w1": nrm(ks[2], (DEPTH, 2, D_MODEL, D_FF), D_MODEL ** -0.5),
        "ffn_w3": nrm(ks[3], (DEPTH, 2, D_MODEL, D_FF), D_MODEL ** -0.5),
        "ffn_w2": nrm(ks[4], (DEPTH, 2, D_FF, D_MODEL), D_FF ** -0.5),
        "ev_w_in": nrm(ks[5], (N_EVEN, D_MODEL, EVEN_IN), D_MODEL ** -0.5),
        "ev_q_gain": 1.0 + nrm(ks[6], (N_EVEN, HEAD_DIM), 0.02),
        "ev_k_gain": 1.0 + nrm(ks[7], (N_EVEN, HEAD_DIM), 0.02),
        "ev_conv_w": nrm(ks[8], (N_EVEN, B_CONV, B_WIDTH), B_CONV ** -0.5),
        "ev_w_out": nrm(ks[9], (N_EVEN, D_MODEL, D_MODEL), D_MODEL ** -0.5),
        "od_w_in": nrm(ks[10], (N_ODD, D_MODEL, ODD_IN), D_MODEL ** -0.5),
        "od_conv_w": nrm(ks[11], (N_ODD, C_CONV, C_WIDTH), C_CONV ** -0.5),
        "od_conv_b": nrm(ks[12], (N_ODD, C_WIDTH), 0.02),
        "od_ln_g": 1.0 + nrm(ks[13], (N_ODD, C_WIDTH), 0.02),
        "od_ln_b": nrm(ks[14], (N_ODD, C_WIDTH), 0.02),
        "od_pool_w": nrm(ks[15], (N_ODD, len(D_WINDOWS), D_GROUP, D_GROUP), D_GROUP ** -0.5),
        "od_pool_scale": 1.0 + nrm(ks[16], (N_ODD, D_WIDTH), 0.02),
        "od_w_out": nrm(ks[17], (N_ODD, D_MODEL, D_MODEL), D_MODEL ** -0.5),
    }


def reference(x, norm_g, ffn_w1, ffn_w3, ffn_w2, ev_w_in, ev_q_gain, ev_k_gain, ev_conv_w,
              ev_w_out, od_w_in, od_conv_w, od_conv_b, od_ln_g, od_ln_b, od_pool_w,
              od_pool_scale, od_w_out):
    for layer in range(DEPTH):
        x = x + 0.5 * _swiglu(_rmsnorm(x, norm_g[layer, 0]),
                              ffn_w1[layer, 0], ffn_w3[layer, 0], ffn_w2[layer, 0])
        xn = _rmsnorm(x, norm_g[layer, 1])
        if layer % 2 == 0:
            i = layer // 2
            x = x + _even_mixer(xn, ev_w_in[i], ev_q_gain[i], ev_k_gain[i],
                                ev_conv_w[i], ev_w_out[i])
        else:
            i = layer // 2
            x = x + _odd_mixer(xn, od_w_in[i], od_conv_w[i], od_conv_b[i], od_ln_g[i],
                               od_ln_b[i], od_pool_w[i], od_pool_scale[i], od_w_out[i])
        x = x + 0.5 * _swiglu(_rmsnorm(x, norm_g[layer, 2]),
                              ffn_w1[layer, 1], ffn_w3[layer, 1], ffn_w2[layer, 1])
    return x
```

```python
import numpy as np
from contextlib import ExitStack
import concourse.bass as bass
import concourse.mybir as mybir
from concourse.bass_utils import run_bass_kernel_spmd

F32 = mybir.dt.float32
BF16 = mybir.dt.bfloat16
AF = mybir.ActivationFunctionType
ALU = mybir.AluOpType

P = 128
D = 2048
DC = D // P
T = 512
FW = 256
EPS = 1e-6
NEG = -30000.0
CONV_PIECE = 16384


class Cfg:
    def __init__(self, S=4096, FF=5632, DEPTH=4, NCORES=8, only=None):
        self.S, self.FF, self.DEPTH, self.NCORES = S, FF, DEPTH, NCORES
        self.only = only
        self.NT = S // T
        self.NF = FF // P
        self.NE = (DEPTH + 1) // 2
        self.NO = DEPTH // 2


class Buf:
    __slots__ = ("name", "lw", "rd", "rd_dma")

    def __init__(self, name=""):
        self.name = name
        self.lw = None
        self.rd = {}
        self.rd_dma = []


class Op:
    __slots__ = ("eng", "fn", "deps", "sig", "tok", "idx", "dma", "pre")

    def __init__(self, eng, fn, dma):
        self.eng, self.fn, self.dma = eng, fn, dma
        self.deps = None
        self.sig = dma
        self.tok = None
        self.pre = None


COMPUTE = ("pe", "act", "dve")
QUEUES = ("sp", "pool")
SEM_LIMIT = 12000
DMA_RING = 12


class Prog:
    def __init__(self):
        self.ops = {e: [] for e in COMPUTE + QUEUES}
        self.barrier_ops = []

    def add(self, eng, fn, reads=(), writes=(), dma=False):
        op = Op(eng, fn, dma)
        deps = {}

        def dep(o):
            if o is None:
                return
            if o.dma:
                deps[id(o)] = o
            else:
                if o.eng == "pe" and eng == "pe":
                    return
                k = o.eng
                cur = deps.get(k)
                if cur is None or cur.idx < o.idx:
                    deps[k] = o

        for b in reads:
            dep(b.lw)
        for b in writes:
            dep(b.lw)
            for o in b.rd.values():
                dep(o)
            for o in b.rd_dma:
                dep(o)
        for o in self.barrier_ops:
            dep(o)
        op.idx = len(self.ops[eng])
        self.ops[eng].append(op)
        op.deps = list(deps.values())
        for o in op.deps:
            o.sig = True
        for b in reads:
            if dma:
                b.rd_dma.append(op)
            else:
                b.rd[eng] = op
        for b in writes:
            b.lw = op
            b.rd = {}
            b.rd_dma = []
        return op

    def barrier(self):
        bo = []
        for e in COMPUTE:
            if self.ops[e]:
                o = self.ops[e][-1]
                o.sig = True
                bo.append(o)
        for q in QUEUES:
            for o in self.ops[q][-DMA_RING:]:
                bo.append(o)
        self.barrier_ops = bo

    def emit(self, nc, es):
        sems = {}
        n_sem = [0]

        def new_sem(tag):
            n_sem[0] += 1
            return es.enter_context(nc.semaphore(f"{tag}{n_sem[0]}"))

        for e in COMPUTE:
            cur = new_sem(e)
            cnt = 0
            for op in self.ops[e]:
                if op.sig:
                    if cnt >= SEM_LIMIT:
                        cur = new_sem(e)
                        cnt = 0
                    cnt += 1
                    op.tok = (cur, cnt, 1)
        for q in QUEUES:
            ring = [[new_sem(q), 0] for _ in range(DMA_RING)]
            for k, op in enumerate(self.ops[q]):
                slot = ring[k % DMA_RING]
                if slot[1] >= SEM_LIMIT:
                    slot[0] = new_sem(q)
                    slot[1] = 0
                op.pre = (slot[0], slot[1]) if slot[1] > 0 else None
                slot[1] += 16
                op.tok = (slot[0], slot[1], 16)
        self.n_sems = n_sem[0]
        self.stats = {e: (len(v), sum(1 for o in v if o.sig)) for e, v in self.ops.items()}

        with nc.Block() as block:
            def run(eng_name, handle):
                waited = {}

                def wait(sem, val):
                    k = id(sem)
                    if waited.get(k, 0) >= val:
                        return
                    waited[k] = val
                    handle.wait_ge(sem, val)

                for op in self.ops[eng_name]:
                    if op.pre is not None:
                        wait(*op.pre)
                    for o in op.deps:
                        wait(o.tok[0], o.tok[1])
                    ins = op.fn(handle)
                    if op.sig:
                        ins.then_inc(op.tok[0], op.tok[2])
                if eng_name in QUEUES:
                    for op in self.ops[eng_name][-DMA_RING:]:
                        wait(op.tok[0], op.tok[1])

            @block.tensor
            def _(h):
                run("pe", h)

            @block.scalar
            def _(h):
                run("act", h)

            @block.vector
            def _(h):
                run("dve", h)

            @block.sync
            def _(h):
                run("sp", h)

            @block.gpsimd
            def _(h):
                run("pool", h)


def tile_in(w):
    K, N = w.shape
    return np.ascontiguousarray(w.reshape(K // P, P, N // FW, FW).transpose(1, 2, 0, 3)).reshape(P, -1)


def tile_w2(w):
    FFn, N = w.shape
    return np.ascontiguousarray(w.reshape(FFn // P, P, N // P, P).transpose(1, 2, 0, 3)).reshape(P, -1)


def vec_cols(v):
    lead = v.shape[:-1]
    n = v.shape[-1] // P
    a = v.reshape(*lead, n, P)
    a = np.moveaxis(a, -1, 0)
    return np.ascontiguousarray(a).reshape(P, -1)


def alibi_exps():
    return list(range(-8, 4))


def const_tables():
    ident = np.eye(P, dtype=np.float32)
    j = np.arange(P)[:, None].astype(np.float32)
    i = np.arange(P)[None, :].astype(np.float32)
    tabs = []
    for e in alibi_exps():
        sl = np.float32(2.0 ** e)
        prev = np.where(j >= i, -sl * (i + P - j), NEG).astype(np.float32)
        cur = np.where(j <= i, -sl * (i - j), NEG).astype(np.float32)
        tabs.append(np.concatenate([prev, cur], axis=1))
    bias = np.concatenate(tabs, axis=1)
    invc = np.tile((1.0 / np.arange(1, 17, dtype=np.float32))[None, :], (P, 1))
    return ident, bias.astype(np.float32), invc.astype(np.float32)


def build_program(cfg):
    S, FF, DEPTH, NT, NF, NE, NO = cfg.S, cfg.FF, cfg.DEPTH, cfg.NT, cfg.NF, cfg.NE, cfg.NO
    NJF = FF // FW
    nc = bass.Bass("TRN2", target_bir_lowering=False)
    pr = Prog()
    es = ExitStack()

    def din(name, shape, dt=F32):
        return nc.dram_tensor(name, list(shape), dt, kind="ExternalInput")

    def dscr(name, shape, dt):
        return nc.dram_tensor(name, list(shape), dt)

    K13 = 16 * FF
    K2 = 16 * FF
    KEI = 16 * 12288
    KO = 16 * 2048
    KOI = 16 * 3072
    KPW = 4 * 2 * 256
    xin = din("xT", [P, DC, S])
    yout = nc.dram_tensor("yT", [P, DC, S], F32, kind="ExternalOutput")
    wsrc = {
        "w1": din("w1", [P, DEPTH * 2 * K13]),
        "w3": din("w3", [P, DEPTH * 2 * K13]),
        "w2": din("w2", [P, DEPTH * 2 * K2]),
        "ewi": din("ewi", [P, NE * KEI]),
        "ewo": din("ewo", [P, NE * KO]),
    }
    if NO:
        wsrc["owi"] = din("owi", [P, NO * KOI])
        wsrc["owo"] = din("owo", [P, NO * KO])
        wsrc["opw"] = din("opw", [P, NO * KPW])
    wbf = {k: dscr(k + "_bf", v.shape, BF16) for k, v in wsrc.items()}
    wbuf = {k: [Buf(f"{k}{i}") for i in range((v.shape[1] + CONV_PIECE - 1) // CONV_PIECE)]
            for k, v in wsrc.items()}

    cvo = {}
    ncv = 0

    def cv_alloc(name, n):
        nonlocal ncv
        cvo[name] = ncv
        ncv += n

    cv_alloc("ng", DEPTH * 3 * DC)
    cv_alloc("qg", NE)
    cv_alloc("kg", NE)
    cv_alloc("ecw", NE * 3 * 8)
    cv_alloc("ocw", max(NO, 1) * 31 * 8)
    cv_alloc("ocb", max(NO, 1) * 8)
    cv_alloc("lng", max(NO, 1) * 8)
    cv_alloc("lnb", max(NO, 1) * 8)
    cv_alloc("psc", max(NO, 1) * 8)
    cv_alloc("invc", 16)
    NCV = ncv
    cvin = din("cvec", [P, NCV])
    identin = din("ident", [P, P])
    biasin = din("biast", [P, 12 * 256])

    QKT = dscr("qkT", [48, P, S], BF16)
    VT = dscr("vT", [24, P, S], BF16)
    H32 = dscr("h32", [24, P, S], F32)
    CATT = dscr("catT", [P, DC, S], BF16)
    U32 = dscr("u32", [P, 8, S], F32)
    PLT = dscr("plT", [P, 8, S], BF16)

    def sb(name, shape, dt):
        return es.enter_context(nc.sbuf_tensor(name, list(shape), dt))

    RA = sb("RA", [P, 8192], F32)
    RB = sb("RB", [P, 8192], BF16)
    RC = sb("RC", [P, 22528], BF16)
    WR = [sb(f"WR{i}", [P, 16 * FW], BF16) for i in range(4)]
    W2R = [sb(f"W2R{i}", [P, NF * P], BF16) for i in range(2)]
    TMP = [sb(f"TMP{i}", [P, T], F32) for i in range(7)]
    LNM = sb("LNM", [P, T], F32)
    LNR = sb("LNR", [P, T], F32)
    bLNM, bLNR = Buf("lnm"), Buf("lnr")
    OUTB = [sb(f"OUTB{i}", [P, T], F32) for i in range(4)]
    XCB = [sb(f"XCB{i}", [P, T], F32) for i in range(2)]
    PTB = [sb(f"PTB{i}", [P, 512], BF16) for i in range(4)]
    CV = sb("CV", [P, NCV], F32)
    QGS = sb("QGS", [P, max(NE, 1)], F32)
    IDENT = sb("IDENT", [P, P], BF16)
    ONES16 = sb("ONES16", [P, P], BF16)
    ONES32 = sb("ONES32", [P, P], F32)
    BIAST = sb("BIAST", [P, 12 * 256], BF16)
    POOLW = sb("POOLW", [P, KPW], BF16)

    PS = [es.enter_context(nc.psum_tensor(f"PS{i}", [P, T], F32)) for i in range(8)]
    bPS = [Buf(f"ps{i}") for i in range(8)]

    bRA, bRB, bRC = Buf("RA"), Buf("RB"), Buf("RC")
    bWR = [Buf(f"wr{i}") for i in range(4)]
    bW2R = [Buf(f"w2r{i}") for i in range(2)]
    bTMP = [Buf(f"tmp{i}") for i in range(7)]
    bOUT = [Buf(f"out{i}") for i in range(4)]
    bXC = [Buf(f"xc{i}") for i in range(2)]
    bPT = [Buf(f"pt{i}") for i in range(4)]
    bCONST = Buf("const")
    bPOOLW = Buf("poolw")
    bXIN = [[Buf() for _ in range(DC)] for _ in range(NT)]
    bY = [[Buf() for _ in range(DC)] for _ in range(NT)]
    bQK = [[Buf() for _ in range(NT)] for _ in range(48)]
    bV = [[Buf() for _ in range(NT)] for _ in range(24)]
    bH = [[Buf() for _ in range(NT)] for _ in range(24)]
    bCAT = [[Buf() for _ in range(NT)] for _ in range(DC)]
    bU = [[Buf() for _ in range(NT)] for _ in range(8)]
    bPL = [[Buf() for _ in range(NT)] for _ in range(8)]

    rot = {"tmp": 0, "out": 0, "xc": 0, "wr": 0, "w2r": 0, "pt": 0}

    def nxt(kind, n):
        v = rot[kind]
        rot[kind] = (v + 1) % n
        return v

    def cvc(name, idx):
        o = cvo[name] + idx
        return CV[:, o:o + 1]

    def dma(q, out, in_, reads, writes):
        return pr.add(q, lambda h: h.dma_start(out=out, in_=in_), reads, writes, dma=True)

    conv_pending = []
    conv_seen = set()
    conv_state = {"every": 1, "cnt": 0, "defer": False, "gate": False}

    def emit_conv(kind, pi):
        tot = wsrc[kind].shape[1]
        a = pi * CONV_PIECE
        b = min(tot, a + CONV_PIECE)
        dma("pool", wbf[kind][:, a:b], wsrc[kind][:, a:b], [], [wbuf[kind][pi]])

    def convert(kind, lo, hi):
        p0 = lo // CONV_PIECE
        p1 = (hi + CONV_PIECE - 1) // CONV_PIECE
        for pi in range(p0, p1):
            if (kind, pi) in conv_seen:
                continue
            conv_seen.add((kind, pi))
            if conv_state["defer"]:
                conv_pending.append((kind, pi))
            else:
                emit_conv(kind, pi)

    def store(out, in_, reads, writes):
        dma("pool", out, in_, reads, writes)
        conv_state["cnt"] += 1
        if conv_state["gate"] and conv_pending and conv_state["cnt"] % conv_state["every"] == 0:
            emit_conv(*conv_pending.pop(0))

    def conv_gate_on():
        conv_state["gate"] = True
        conv_state["cnt"] = 0

    def conv_flush():
        conv_state["gate"] = False
        while conv_pending:
            emit_conv(*conv_pending.pop(0))

    def wread(kind, off, n):
        return wbuf[kind][off // CONV_PIECE:(off + n - 1) // CONV_PIECE + 1]

    def load_w(kind, off, n, dst_ap, dst_buf):
        dma("sp", dst_ap, wbf[kind][:, off:off + n], wread(kind, off, n), [dst_buf])

    def act(out, in_, func, reads, writes, bias=None, scale=None):
        kw = {}
        if bias is not None:
            kw["bias"] = bias
        if scale is not None:
            kw["scale"] = scale
        return pr.add("act", lambda h: h.activation(out=out, in_=in_, func=func, **kw), reads, writes)

    def stt(out, in0, scalar, in1, op0, op1, reads, writes):
        return pr.add("dve", lambda h: h.scalar_tensor_tensor(out=out, in0=in0, scalar=scalar, in1=in1,
                                                              op0=op0, op1=op1), reads, writes)

    def tt(out, in0, in1, op, reads, writes):
        return pr.add("dve", lambda h: h.tensor_tensor(out=out, in0=in0, in1=in1, op=op), reads, writes)

    def tsc(out, in0, s1, s2, op0, op1, reads, writes):
        if op1 is None:
            return pr.add("dve", lambda h: h.tensor_scalar(out=out, in0=in0, scalar1=s1, scalar2=None,
                                                           op0=op0), reads, writes)
        return pr.add("dve", lambda h: h.tensor_scalar(out=out, in0=in0, scalar1=s1, scalar2=s2,
                                                       op0=op0, op1=op1), reads, writes)

    def recip(ap, buf):
        return pr.add("dve", lambda h: h.reciprocal(out=ap, in_=ap), [buf], [buf])

    def mm_group(out, pairs, reads, writes):
        n = len(pairs)

        def fn(h):
            ins = None
            for i, (l, r) in enumerate(pairs):
                ins = h.matmul(out, l, r, start=(i == 0), stop=(i == n - 1))
            return ins
        return pr.add("pe", fn, reads, writes)

    pr.add("dve", lambda h: h.memset(ONES32[:], 1.0), [], [bCONST])
    pr.add("dve", lambda h: h.memset(ONES16[:], 1.0), [], [bCONST])
    dma("sp", CV[:], cvin[:, :], [], [bCONST])
    dma("pool", IDENT[:], identin[:, :], [], [bCONST])
    dma("pool", BIAST[:], biasin[:, :], [], [bCONST])
    if NE:
        tsc(QGS[:, 0:NE], CV[:, cvo["qg"]:cvo["qg"] + NE], float(P ** -0.5), None, ALU.mult, None,
            [bCONST], [bCONST])

    XA = RA[:].rearrange("p (c t) -> p c t", c=DC)
    XN = RB[:].rearrange("p (c t) -> p c t", c=DC)
    bXNc = [Buf(f"xn{c}") for c in range(DC)]

    def x_src(first):
        return (xin, bXIN) if first else (yout, bY)

    XN2 = RC[:, 0:DC * T].rearrange("p (c t) -> p c t", c=DC)
    bXN2c = [Buf(f"xnb{c}") for c in range(DC)]
    RSTD = sb("RSTD", [P, T], F32)
    bRSTD = Buf("rstd")
    SQB = [sb(f"SQB{i}", [P, T], BF16) for i in range(8)]
    bSQB = [Buf(f"sqb{i}") for i in range(8)]
    sq_state = {"c": DC, "mm": DC}

    def prologue_begin(i, src, srcb):
        t0 = i * T
        dma("sp", XA, src[:, :, t0:t0 + T], srcb[i], [bRA])
        sq_state["c"] = 0
        sq_state["mm"] = 0

    def prologue_step():
        while sq_state["mm"] < sq_state["c"]:
            c = sq_state["mm"]
            pr.add("pe", (lambda c=c: lambda h: h.matmul(PS[6][:], ONES16[:], SQB[c % 8][:],
                                                         start=(c == 0), stop=(c == DC - 1)))(),
                   [bSQB[c % 8], bCONST], [bPS[6]])
            sq_state["mm"] += 1
        n = 0
        while sq_state["c"] < DC and n < 8:
            c = sq_state["c"]
            act(SQB[c % 8][:], XA[:, c, :], AF.Square, [bRA], [bSQB[c % 8]])
            sq_state["c"] += 1
            n += 1

    def prologue_busy():
        return sq_state["mm"] < DC

    def prologue_flush():
        while prologue_busy():
            prologue_step()

    def prologue_norm(gidx, xn_view, xn_bufs):
        act(RSTD[:], PS[6][:], AF.Sqrt, [bPS[6]], [bRSTD], bias=EPSB[:, 0:1], scale=1.0 / D)
        recip(RSTD[:], bRSTD)
        for c in range(DC):
            stt(xn_view[:, c, :], XA[:, c, :], cvc("ng", gidx * DC + c), RSTD[:], ALU.mult, ALU.mult,
                [bRA, bRSTD, bCONST], [xn_bufs[c]])

    EPSB = sb("EPSB", [P, 1], F32)
    pr.add("dve", lambda h: h.memset(EPSB[:], EPS), [], [bCONST])
    pr.barrier()

    G = RC[:, 0:NF * T].rearrange("p (f t) -> p f t", f=NF)
    bGc = [Buf(f"g{f}") for f in range(NF)]

    def ffn(l, s, first):
        src, srcb = x_src(first)
        gidx = l * 3 + (0 if s == 0 else 2)
        base13 = (l * 2 + s) * K13
        base2 = (l * 2 + s) * K2
        jm = NJF // 2
        prologue_begin(0, src, srcb)
        prologue_flush()
        prologue_norm(gidx, XN, bXNc)
        for i in range(NT):
            t0 = i * T
            for j in range(NJF):
                if i + 1 < NT and j == jm:
                    prologue_begin(i + 1, src, srcb)
                if i + 1 < NT and j >= jm and prologue_busy():
                    prologue_step()
                k1 = nxt("wr", 4)
                load_w("w1", base13 + j * 16 * FW, 16 * FW, WR[k1][:], bWR[k1])
                k3 = nxt("wr", 4)
                load_w("w3", base13 + j * 16 * FW, 16 * FW, WR[k3][:], bWR[k3])
                w1v = WR[k1][:].rearrange("p (k f) -> p k f", k=16)
                w3v = WR[k3][:].rearrange("p (k f) -> p k f", k=16)
                for m in range(FW // P):
                    fc = j * (FW // P) + m
                    b1 = fc % 2
                    b3 = 2 + fc % 2
                    mm_group(PS[b1][:], [(w1v[:, kc, m * P:(m + 1) * P], XN[:, kc, :]) for kc in range(DC)],
                             [bWR[k1]] + bXNc, [bPS[b1]])
                    mm_group(PS[b3][:], [(w3v[:, kc, m * P:(m + 1) * P], XN[:, kc, :]) for kc in range(DC)],
                             [bWR[k3]] + bXNc, [bPS[b3]])
                    kt = nxt("tmp", 7)
                    act(TMP[kt][:], PS[b1][:], AF.Silu, [bPS[b1]], [bTMP[kt]])
                    tt(G[:, fc, :], PS[b3][:], TMP[kt][:], ALU.mult, [bPS[b3], bTMP[kt]], [bGc[fc]])
            if i + 1 < NT:
                prologue_flush()
                prologue_norm(gidx, XN, bXNc)
            for dc in range(DC):
                k2 = nxt("w2r", 2)
                load_w("w2", base2 + dc * NF * P, NF * P, W2R[k2][:], bW2R[k2])
                w2v = W2R[k2][:].rearrange("p (f d) -> p f d", f=NF)
                kx = nxt("xc", 2)
                dma("sp", XCB[kx][:], src[:, dc, t0:t0 + T], [srcb[i][dc]], [bXC[kx]])
                yb = 4 + dc % 2
                mm_group(PS[yb][:], [(w2v[:, fc, :], G[:, fc, :]) for fc in range(NF)],
                         [bW2R[k2]] + bGc, [bPS[yb]])
                ko = nxt("out", 4)
                stt(OUTB[ko][:], PS[yb][:], 0.5, XCB[kx][:], ALU.mult, ALU.add,
                    [bPS[yb], bXC[kx]], [bOUT[ko]])
                store(yout[:, dc, t0:t0 + T], OUTB[ko][:], [bOUT[ko]], [bY[i][dc]])

    def projection(l, kind, base, ncol_chunks, chunk_kind, gain_ap):
        gidx = l * 3 + 1
        per_tile = FW // P
        xnb = [(XN, bXNc), (XN2, bXN2c)]
        NJ = ncol_chunks // per_tile
        jm = NJ // 2
        prologue_begin(0, yout, bY)
        prologue_flush()
        prologue_norm(gidx, *xnb[0])
        for i in range(NT):
            t0 = i * T
            XNi, bXNi = xnb[i % 2]
            pend = []

            def qk_tail(t0=t0):
                for (cc_, pb_, ck_, kt_, sq16_) in pend:
                    sb_ = 6 if False else 7
                    pr.add("pe", (lambda sq16_=sq16_, sb_=sb_: lambda h: h.matmul(
                        PS[sb_][:], ONES16[:], sq16_, start=True, stop=True))(),
                        [bTMP[kt_], bCONST], [bPS[sb_]])
                    kr = nxt("tmp", 7)
                    act(TMP[kr][:], PS[sb_][:], AF.Sqrt, [bPS[sb_]], [bTMP[kr]],
                        bias=EPSB[:, 0:1], scale=1.0 / P)
                    recip(TMP[kr][:], bTMP[kr])
                    ko_ = nxt("out", 4)
                    o16 = OUTB[ko_][:].bitcast(BF16)[:, 0:T]
                    stt(o16, PS[pb_][:], ck_[2], TMP[kr][:], ALU.mult, ALU.mult,
                        [bPS[pb_], bTMP[kr], bCONST], [bOUT[ko_]])
                    store(QKT[ck_[1], :, t0:t0 + T], o16, [bOUT[ko_]], [bQK[ck_[1]][i]])
                del pend[:]

            normed = [False]
            if kind == "ewi":
                half = NJ // 2
                order = [x for pair in zip(range(half), range(half, NJ)) for x in pair]
            else:
                order = list(range(NJ))
            pcnt = 0
            for jpos, j in enumerate(order):
                if i + 1 < NT and jpos == jm:
                    prologue_begin(i + 1, yout, bY)
                if i + 1 < NT and jpos >= jm and prologue_busy():
                    prologue_step()
                    if not prologue_busy():
                        prologue_norm(gidx, *xnb[(i + 1) % 2])
                        normed[0] = True
                kw = nxt("wr", 4)
                load_w(kind, base + j * 16 * FW, 16 * FW, WR[kw][:], bWR[kw])
                wv = WR[kw][:].rearrange("p (k f) -> p k f", k=16)
                for m in range(per_tile):
                    cc = j * per_tile + m
                    pb = pcnt % 4
                    pcnt += 1
                    mm_group(PS[pb][:], [(wv[:, kc, m * P:(m + 1) * P], XNi[:, kc, :]) for kc in range(DC)],
                             [bWR[kw]] + bXNi, [bPS[pb]])
                    ck = chunk_kind(cc)
                    if ck[0] == "qk":
                        kt = nxt("tmp", 7)
                        sq16 = TMP[kt][:].bitcast(BF16)[:, 0:T]
                        act(sq16, PS[pb][:], AF.Square, [bPS[pb]], [bTMP[kt]])
                        qk_tail()
                        pend.append((cc, pb, ck, kt, sq16))
                    else:
                        qk_tail()
                        ko = nxt("out", 4)
                        if ck[0] == "v":
                            o16 = OUTB[ko][:].bitcast(BF16)[:, 0:T]
                            act(o16, PS[pb][:], AF.Copy, [bPS[pb]], [bOUT[ko]])
                            store(VT[ck[1], :, t0:t0 + T], o16, [bOUT[ko]], [bV[ck[1]][i]])
                        else:
                            act(OUTB[ko][:], PS[pb][:], AF.Copy, [bPS[pb]], [bOUT[ko]])
                            store(H32[ck[1], :, t0:t0 + T], OUTB[ko][:], [bOUT[ko]], [bH[ck[1]][i]])
            qk_tail()
            if i + 1 < NT and not normed[0]:
                prologue_flush()
                prologue_norm(gidx, *xnb[(i + 1) % 2])

    CAT = RB[:].rearrange("p (c t) -> p c t", c=DC)
    bCATc = [Buf(f"cat{c}") for c in range(DC)]
    CAT2 = RC[:, 0:DC * T].rearrange("p (c t) -> p c t", c=DC)
    bCAT2c = [Buf(f"catb{c}") for c in range(DC)]
    CATS = [(CAT, bCATc), (CAT2, bCAT2c)]

    def out_proj(kind, base, i, CATv=None, bCATv=None):
        CATv = CAT if CATv is None else CATv
        bCATv = bCATc if bCATv is None else bCATv
        t0 = i * T
        per_tile = FW // P
        for j in range(DC // per_tile):
            kw = nxt("wr", 4)
            load_w(kind, base + j * 16 * FW, 16 * FW, WR[kw][:], bWR[kw])
            wv = WR[kw][:].rearrange("p (k f) -> p k f", k=16)
            for m in range(per_tile):
                dc = j * per_tile + m
                pb = dc % 4
                kx = nxt("xc", 2)
                dma("sp", XCB[kx][:], yout[:, dc, t0:t0 + T], [bY[i][dc]], [bXC[kx]])
                mm_group(PS[pb][:], [(wv[:, kc, m * P:(m + 1) * P], CATv[:, kc, :]) for kc in range(DC)],
                         [bWR[kw]] + bCATv, [bPS[pb]])
                ko = nxt("out", 4)
                tt(OUTB[ko][:], PS[pb][:], XCB[kx][:], ALU.add, [bPS[pb], bXC[kx]], [bOUT[ko]])
                store(yout[:, dc, t0:t0 + T], OUTB[ko][:], [bOUT[ko]], [bY[i][dc]])

    def even_mixer(l):
        ie = l // 2
        qg = QGS[:, ie:ie + 1]
        kg = cvc("kg", ie)

        def ck(cc):
            if cc < 24:
                return ("qk", cc, qg)
            if cc < 48:
                return ("qk", cc, kg)
            if cc < 72:
                return ("v", cc - 48)
            return ("h", cc - 72)
        projection(l, "ewi", ie * KEI, 96, ck, None)
        pr.barrier()
        conv_gate_on()
        ACO = RA[:, 0:S]
        ACD = RA[:, S:2 * S]
        bACO, bACD = Buf("aco"), Buf("acd")
        qkv = []
        for b in range(2):
            if b == 0:
                qkv.append((RB[:, 0:S], RB[:, S:2 * S], RC[:, 0:S]))
            else:
                qkv.append((RC[:, S:2 * S], RC[:, 2 * S:3 * S], RC[:, 3 * S:4 * S]))
        bq = [[Buf(), Buf(), Buf()] for _ in range(2)]
        NBLK = S // P
        VBLK = RC[:, 4 * S:5 * S].rearrange("p (b d) -> p b d", b=NBLK)
        bVB = Buf("vblk")
        YA = RC[:, 5 * S:5 * S + 4 * T].rearrange("p (k t) -> p k t", k=4)
        bYA = [Buf() for _ in range(4)]
        ps7b = PS[7][:].bitcast(BF16)
        ps6b = PS[6][:].bitcast(BF16)
        SBK = [0, 1, 6]
        def attn_hg(h, g, qb):
            d = 4 ** g
            e_idx = (2 * g - (h + 1)) + 8
            bprev = BIAST[:, e_idx * 256:e_idx * 256 + 128]
            bcur = BIAST[:, e_idx * 256 + 128:e_idx * 256 + 256]
            QTv, KTv, VTv = qkv[qb]
            hq, hk = g * 8 + h, 24 + g * 8 + h
            dma("sp", QTv, QKT[hq, :, :], bQK[hq], [bq[qb][0]])
            dma("sp", KTv, QKT[hk, :, :], bQK[hk], [bq[qb][1]])
            dma("sp", VTv, VT[g * 8 + h, :, :], bV[g * 8 + h], [bq[qb][2]])
            NB = S // (P * d)
            blocks = [(n, r) for n in range(NB) for r in range(d)]

            def tok(n, r):
                st_ = P * n * d + r
                return slice(st_, st_ + (P - 1) * d + 1, d)

            for rnd, b0 in enumerate(range(0, NBLK, 8)):
                tb = 7 if rnd % 2 == 0 else 6
                tview = ps7b if tb == 7 else ps6b

                def tfn(h_, b0=b0, tview=tview):
                    ins = None
                    for bi in range(8):
                        n, r = blocks[b0 + bi]
                        ins = h_.transpose(tview[:, bi * P:(bi + 1) * P], VTv[:, tok(n, r)], IDENT[:])
                    return ins
                pr.add("pe", tfn, [bq[qb][2], bCONST], [bPS[tb]])
                pr.add("act", (lambda b0=b0, tview=tview: lambda h_: h_.activation(
                    out=VBLK[:, b0:b0 + 8, :], in_=tview[:, 0:8 * P].rearrange("p (b d) -> p b d", b=8),
                    func=AF.Copy))(), [bPS[tb]], [bVB])

            def scores(sbi):
                bank = SBK[sbi % 3]

                def fn(h_):
                    ins = None
                    for k in range(2):
                        n, r = blocks[2 * sbi + k]
                        off = k * 256
                        qs = QTv[:, tok(n, r)]
                        h_.matmul(PS[bank][:, off + 128:off + 256], KTv[:, tok(n, r)], qs, start=True, stop=False)
                        ins = h_.matmul(PS[bank][:, off + 128:off + 256], IDENT[:], bcur, start=False, stop=True)
                        if n > 0:
                            h_.matmul(PS[bank][:, off:off + 128], KTv[:, tok(n - 1, r)], qs, start=True, stop=False)
                            ins = h_.matmul(PS[bank][:, off:off + 128], IDENT[:], bprev, start=False, stop=True)
                    return ins
                pr.add("pe", fn, [bq[qb][0], bq[qb][1], bCONST], [bPS[bank]])

            def expo(sbi):
                bank = SBK[sbi % 3]
                kp = sbi % 4
                n0 = blocks[2 * sbi][0]
                n1 = blocks[2 * sbi + 1][0]
                if n0 > 0 and n1 > 0:
                    act(PTB[kp][:, 0:512], PS[bank][:, 0:512], AF.Exp, [bPS[bank]], [bPT[kp]])
                else:
                    for k, n in ((0, n0), (1, n1)):
                        lo = k * 256 + (0 if n > 0 else 128)
                        hi = k * 256 + 256
                        act(PTB[kp][:, lo:hi], PS[bank][:, lo:hi], AF.Exp, [bPS[bank]], [bPT[kp]])

            def pv(sbi):
                kp = sbi % 4
                grp = sbi // 2
                ob = 2 + grp % 2
                db = 4 + grp % 2

                def fn(h_):
                    ins = None
                    for k in range(2):
                        bi = 2 * sbi + k
                        n, r = blocks[bi]
                        col = (bi % 4) * P
                        off = k * 256
                        h_.matmul(PS[ob][:, col:col + P], VBLK[:, bi, :], PTB[kp][:, off + 128:off + 256],
                                  start=True, stop=(n == 0))
                        if n > 0:
                            h_.matmul(PS[ob][:, col:col + P], VBLK[:, bi - d, :], PTB[kp][:, off:off + 128],
                                      start=False, stop=True)
                        ins = h_.matmul(PS[db][:, col:col + P], ONES16[:], PTB[kp][:, off + 128:off + 256],
                                        start=True, stop=(n == 0))
                        if n > 0:
                            ins = h_.matmul(PS[db][:, col:col + P], ONES16[:], PTB[kp][:, off:off + 128],
                                            start=False, stop=True)
                    return ins
                pr.add("pe", fn, [bPT[kp], bVB, bCONST], [bPS[ob], bPS[db]])

            def accum(grp):
                n, r0 = blocks[grp * 4]
                ob = 2 + grp % 2
                db = 4 + grp % 2
                if d == 1:
                    sl = slice(grp * 4 * P, grp * 4 * P + 4 * P)
                    dsto, dstd = ACO[:, sl], ACD[:, sl]
                    srco, srcd = PS[ob][:], PS[db][:]
                elif d == 4:
                    sl = slice(n * 512, n * 512 + 512)
                    dsto = ACO[:, sl].rearrange("p (i r) -> p r i", r=4)
                    dstd = ACD[:, sl].rearrange("p (i r) -> p r i", r=4)
                    srco = PS[ob][:].rearrange("p (r i) -> p r i", r=4)
                    srcd = PS[db][:].rearrange("p (r i) -> p r i", r=4)
                else:
                    sl = slice(n * 2048, n * 2048 + 2048)
                    dsto = ACO[:, sl].rearrange("p (i r) -> p r i", r=16)[:, r0:r0 + 4, :]
                    dstd = ACD[:, sl].rearrange("p (i r) -> p r i", r=16)[:, r0:r0 + 4, :]
                    srco = PS[ob][:].rearrange("p (r i) -> p r i", r=4)
                    srcd = PS[db][:].rearrange("p (r i) -> p r i", r=4)
                if g == 0:
                    pr.add("dve", lambda h_: h_.tensor_copy(out=dsto, in_=srco), [bPS[ob]], [bACO])
                    pr.add("dve", lambda h_: h_.tensor_copy(out=dstd, in_=srcd), [bPS[db]], [bACD])
                else:
                    tt(dsto, srco, dsto, ALU.add, [bPS[ob], bACO], [bACO])
                    tt(dstd, srcd, dstd, ALU.add, [bPS[db], bACD], [bACD])

            NSB = NBLK // 2
            scores(0)
            scores(1)
            for sbi in range(NSB):
                if sbi + 2 < NSB:
                    scores(sbi + 2)
                expo(sbi)
                pv(sbi)
                if sbi % 2 == 1:
                    accum(sbi // 2)

        it = 0
        for h in range(8):
            for g in range(3):
                attn_hg(h, g, it % 2)
                it += 1
            for i in range(NT):
                sl = slice(i * T, (i + 1) * T)
                recip(ACD[:, sl], bACD)
                ky = i % 4
                tt(YA[:, ky, :], ACO[:, sl], ACD[:, sl], ALU.mult, [bACO, bACD], [bYA[ky]])
                store(CATT[:, h, sl], YA[:, ky, :], [bYA[ky]], [bCAT[h][i]])
        pr.barrier()
        RCf = RC[:].bitcast(F32)
        CG32 = [RA[:, 0:S], RCf[:, 0:S]]
        XT32 = [RA[:, S:2 * S], RCf[:, S:2 * S]]
        UB = [RB[:, 0:S + 32], RC[:, 4 * S:4 * S + S + 32]]
        D3 = [WR[0][:, 0:3 * P].rearrange("p (j m) -> p j m", j=3),
              WR[1][:, 0:3 * P].rearrange("p (j m) -> p j m", j=3)]
        bCG, bXT, bUBb, bD = [Buf(), Buf()], [Buf(), Buf()], [Buf(), Buf()], [Buf(), Buf()]
        for s_ in range(2):
            pr.add("dve", (lambda s_=s_: lambda h_: h_.memset(UB[s_][:, 0:32], 0.0))(), [], [bUBb[s_]])
        def e3_prep(c):
            s_ = c % 2
            dma("sp", CG32[s_], H32[8 + c, :, :], bH[8 + c], [bCG[s_]])
            dma("sp", XT32[s_], H32[16 + c, :, :], bH[16 + c], [bXT[s_]])
            tt(UB[s_][:, 32:32 + S], CG32[s_], XT32[s_], ALU.mult, [bCG[s_], bXT[s_]], [bUBb[s_]])
            for tp in range(3):
                tsc(D3[s_][:, tp, :], IDENT[:], cvc("ecw", (ie * 3 + tp) * 8 + c), None, ALU.mult, None,
                    [bCONST], [bD[s_]])

        e3_prep(0)
        for c in range(8):
            s_ = c % 2
            if c + 1 < 8:
                e3_prep(c + 1)
            for i in range(NT):
                t0 = i * T
                pb = i % 4
                kx = nxt("xc", 2)
                dma("sp", XCB[kx][:], H32[c, :, t0:t0 + T], [bH[c][i]], [bXC[kx]])
                mm_group(PS[pb][:], [(D3[s_][:, tp, :], UB[s_][:, 32 + t0 - (2 - tp):32 + t0 - (2 - tp) + T])
                                     for tp in range(3)], [bD[s_], bUBb[s_]], [bPS[pb]])
                ko = nxt("out", 4)
                o16 = OUTB[ko][:].bitcast(BF16)[:, 0:T]
                tt(o16, PS[pb][:], XCB[kx][:], ALU.mult, [bPS[pb], bXC[kx]], [bOUT[ko]])
                store(CATT[:, 8 + c, t0:t0 + T], o16, [bOUT[ko]], [bCAT[8 + c][i]])
        pr.barrier()
        def e4_load(i):
            cv, cb = CATS[i % 2]
            dma("sp", cv, CATT[:, :, i * T:(i + 1) * T], [bCAT[c][i] for c in range(DC)], cb)
        e4_load(0)
        for i in range(NT):
            if i + 1 < NT:
                e4_load(i + 1)
            out_proj("ewo", ie * KO, i, *CATS[i % 2])
        pr.barrier()

    def odd_mixer(l):
        io = l // 2
        projection(l, "owi", io * KOI, 24, lambda cc: ("h", cc), None)
        load_w("opw", io * KPW, KPW, POOLW[:], bPOOLW)
        pr.barrier()
        conv_gate_on()
        RCf = RC[:].bitcast(F32)
        A32 = [RA[:, 0:S], RCf[:, 0:S]]
        G32 = [RA[:, S:2 * S], RCf[:, S:2 * S]]
        UB = [RB[:, 0:S + 32], RC[:, 4 * S:4 * S + S + 32]]
        D31 = [WR[0][:, 0:31 * P].rearrange("p (j m) -> p j m", j=31),
               WR[1][:, 0:31 * P].rearrange("p (j m) -> p j m", j=31)]
        bA, bG, bUBb, bD = [Buf(), Buf()], [Buf(), Buf()], [Buf(), Buf()], [Buf(), Buf()]
        for s_ in range(2):
            pr.add("dve", (lambda s_=s_: lambda h_: h_.memset(UB[s_][:, 0:32], 0.0))(), [], [bUBb[s_]])
        def o2_prep(c):
            s_ = c % 2
            dma("sp", A32[s_], H32[c, :, :], bH[c], [bA[s_]])
            dma("sp", G32[s_], H32[8 + c, :, :], bH[8 + c], [bG[s_]])
            act(G32[s_], G32[s_], AF.Sigmoid, [bG[s_]], [bG[s_]])
            tt(UB[s_][:, 32:32 + S], A32[s_], G32[s_], ALU.mult, [bA[s_], bG[s_]], [bUBb[s_]])
            for jt in range(31):
                tsc(D31[s_][:, jt, :], IDENT[:], cvc("ocw", (io * 31 + jt) * 8 + c), None, ALU.mult, None,
                    [bCONST], [bD[s_]])

        HW_ = T + 16
        W2f = WR[2][:].bitcast(F32)
        W3f = WR[3][:].bitcast(F32)
        ZH = [W2f[:, 0:HW_], W2f[:, HW_:2 * HW_]]
        SAo = W2f[:, 2 * HW_:3 * HW_]
        SBo = W3f[:, 0:HW_]
        tmpc = W3f[:, HW_:HW_ + 16]
        bZH, bSAo, bSBo, bTc = [Buf(), Buf()], Buf(), Buf(), Buf()
        io_ = cvo["invc"]
        zk = [0]

        def o3_tile(c, i):
            g = c // 2
            kwin = 2 ** (g + 1)
            t0 = i * T
            k = zk[0] % 2
            zk[0] += 1
            if i == 0:
                pr.add("dve", (lambda k=k: lambda h_: h_.memset(ZH[k][:, 0:16], 0.0))(), [], [bZH[k]])
                dma("sp", ZH[k][:, 16:HW_], H32[16 + c, :, 0:T], [bH[16 + c][0]], [bZH[k]])
            else:
                dma("sp", ZH[k][:, 0:HW_], H32[16 + c, :, t0 - 16:t0 + T], [bH[16 + c][i - 1], bH[16 + c][i]],
                    [bZH[k]])
            cur, curb = ZH[k], bZH[k]
            pp = [(SAo, bSAo), (SBo, bSBo)]
            for stp in range(g + 1):
                kk = 2 ** stp
                dst, dstb = pp[stp % 2]
                pr.add("dve", (lambda dst=dst, cur=cur, kk=kk: lambda h_: h_.tensor_copy(
                    out=dst[:, 0:kk], in_=cur[:, 0:kk]))(), [curb], [dstb])
                tt(dst[:, kk:HW_], cur[:, kk:HW_], cur[:, 0:HW_ - kk], ALU.add, [curb], [dstb])
                cur, curb = dst, dstb
            ko = nxt("out", 4)
            o16 = OUTB[ko][:].bitcast(BF16)[:, 0:T]
            stt(o16, cur[:, 16:HW_], 1.0 / kwin, ZH[k][:, 16:HW_], ALU.mult, ALU.subtract,
                [curb, bZH[k]], [bOUT[ko]])
            if i == 0:
                nfix = kwin - 1
                tt(tmpc[:, 0:nfix], cur[:, 16:16 + nfix], CV[:, io_:io_ + nfix], ALU.mult, [curb, bCONST], [bTc])
                tt(o16[:, 0:nfix], tmpc[:, 0:nfix], ZH[k][:, 16:16 + nfix], ALU.subtract, [bTc, bZH[k]],
                   [bOUT[ko]])
            store(PLT[:, c, t0:t0 + T], o16, [bOUT[ko]], [bPL[c][i]])

        o2_prep(0)
        for c in range(8):
            s_ = c % 2
            if c + 1 < 8:
                o2_prep(c + 1)
            for i in range(NT):
                t0 = i * T
                pb = i % 4
                mm_group(PS[pb][:], [(D31[s_][:, jt, :], UB[s_][:, 32 + t0 - (30 - jt):32 + t0 - (30 - jt) + T])
                                     for jt in range(31)], [bD[s_], bUBb[s_]], [bPS[pb]])
                ko = nxt("out", 4)
                act(OUTB[ko][:], PS[pb][:], AF.Identity, [bPS[pb], bCONST], [bOUT[ko]],
                    bias=cvc("ocb", io * 8 + c))
                store(U32[:, c, t0:t0 + T], OUTB[ko][:], [bOUT[ko]], [bU[c][i]])
                o3_tile(c, i)
        pr.barrier()
        UT = RA[:, 0:8 * T].rearrange("p (c t) -> p c t", c=8)
        PLt = RA[:, 8 * T:16 * T].bitcast(BF16)[:, 0:8 * T].rearrange("p (c t) -> p c t", c=8)
        bUT, bPLt = Buf(), Buf()
        pwv = POOLW[:].rearrange("p (g c e) -> p g c e", g=4, c=2)

        def o4_ln(i):
            t0 = i * T
            CATv, bCATv = CATS[i % 2]
            dma("sp", UT, U32[:, :, t0:t0 + T], [bU[c][i] for c in range(8)], [bUT])
            dma("sp", PLt, PLT[:, :, t0:t0 + T], [bPL[c][i] for c in range(8)], [bPLt])
            for c in range(8):
                pr.add("pe", (lambda c=c: lambda h_: h_.matmul(PS[6][:], ONES32[:], UT[:, c, :],
                                                               start=(c == 0), stop=(c == 7)))(),
                       [bUT, bCONST], [bPS[6]])
            for c in range(8):
                k = nxt("tmp", 7)
                act(TMP[k][:], UT[:, c, :], AF.Square, [bUT], [bTMP[k]])
                pr.add("pe", (lambda c=c, k=k: lambda h_: h_.matmul(PS[7][:], ONES32[:], TMP[k][:],
                                                                    start=(c == 0), stop=(c == 7)))(),
                       [bTMP[k], bCONST], [bPS[7]])
            act(LNM[:], PS[6][:], AF.Identity, [bPS[6]], [bLNM], scale=1.0 / 1024)
            tt(LNR[:], LNM[:], LNM[:], ALU.mult, [bLNM], [bLNR])
            stt(LNR[:], PS[7][:], 1.0 / 1024, LNR[:], ALU.mult, ALU.subtract,
                [bPS[7], bLNR], [bLNR])
            act(LNR[:], LNR[:], AF.Sqrt, [bLNR], [bLNR], bias=EPSB[:, 0:1], scale=1.0)
            recip(LNR[:], bLNR)
            for c in range(8):
                k1 = nxt("tmp", 7)
                tt(TMP[k1][:], UT[:, c, :], LNM[:], ALU.subtract, [bUT, bLNM], [bTMP[k1]])
                stt(TMP[k1][:], TMP[k1][:], cvc("lng", io * 8 + c), LNR[:], ALU.mult, ALU.mult,
                    [bTMP[k1], bLNR, bCONST], [bTMP[k1]])
                act(CATv[:, c, :], TMP[k1][:], AF.Silu, [bTMP[k1], bCONST], [bCATv[c]],
                    bias=cvc("lnb", io * 8 + c))
            for e in range(8):
                g, ec = e // 2, e % 2
                pb = e % 4
                mm_group(PS[pb][:], [(pwv[:, g, cc, ec * P:(ec + 1) * P], PLt[:, 2 * g + cc, :])
                                     for cc in range(2)], [bPOOLW, bPLt], [bPS[pb]])
                act(CATv[:, 8 + e, :], PS[pb][:], AF.Identity, [bPS[pb], bCONST], [bCATv[8 + e]],
                    scale=cvc("psc", io * 8 + e))

        o4_ln(0)
        for i in range(NT):
            if i + 1 < NT:
                o4_ln(i + 1)
            out_proj("owo", io * KO, i, *CATS[i % 2])
        pr.barrier()

    def conv_ffn(l, s):
        lo, hi = (l * 2 + s) * K13, (l * 2 + s + 1) * K13
        a = lo
        while a < hi:
            b = min(hi, (a // CONV_PIECE + 1) * CONV_PIECE)
            convert("w1", a, b)
            convert("w3", a, b)
            a = b
        convert("w2", (l * 2 + s) * K2, (l * 2 + s + 1) * K2)

    def conv_mixer(l):
        if l % 2 == 0:
            convert("ewi", (l // 2) * KEI, (l // 2 + 1) * KEI)
            convert("ewo", (l // 2) * KO, (l // 2 + 1) * KO)
        else:
            convert("owi", (l // 2) * KOI, (l // 2 + 1) * KOI)
            convert("owo", (l // 2) * KO, (l // 2 + 1) * KO)
            convert("opw", (l // 2) * KPW, (l // 2 + 1) * KPW)

    stages = []
    for l in range(DEPTH):
        stages.append(("ffn", l, 0))
        stages.append(("mix", l, 0))
        stages.append(("ffn", l, 1))

    def conv_stage(st):
        if st[0] == "ffn":
            conv_ffn(st[1], st[2])
        else:
            conv_mixer(st[1])

    if cfg.only is not None:
        stages = [tuple(s) for s in cfg.only]
    def n_stores(st):
        if st[0] == "ffn":
            return DC * NT
        if st[1] % 2 == 0:
            return 96 * NT
        return 24 * NT

    conv_stage(stages[0])
    for si, st in enumerate(stages):
        if si + 1 < len(stages):
            if st[0] == "mix":
                conv_state["defer"] = True
                conv_stage(stages[si + 1])
                conv_state["defer"] = False
                conv_state["every"] = max(1, int(0.8 * 72 / max(1, len(conv_pending))))
            else:
                conv_stage(stages[si + 1])
        if st[0] == "ffn":
            ffn(st[1], st[2], first=(si == 0))
            pr.barrier()
        elif st[1] % 2 == 0:
            even_mixer(st[1])
        else:
            odd_mixer(st[1])
        conv_flush()

    pr.emit(nc, es)
    es.close()
    nc._prog_stats = (pr.n_sems, pr.stats)
    return nc, NCV, cvo


def prep_shared(cfg, inp):
    DEPTH, NE, NO = cfg.DEPTH, cfg.NE, cfg.NO
    sh = {}
    sh["w1"] = np.concatenate([tile_in(inp["ffn_w1"][l, s]) for l in range(DEPTH) for s in range(2)], axis=1)
    sh["w3"] = np.concatenate([tile_in(inp["ffn_w3"][l, s]) for l in range(DEPTH) for s in range(2)], axis=1)
    sh["w2"] = np.concatenate([tile_w2(inp["ffn_w2"][l, s]) for l in range(DEPTH) for s in range(2)], axis=1)
    sh["ewi"] = np.concatenate([tile_in(inp["ev_w_in"][i]) for i in range(NE)], axis=1)
    sh["ewo"] = np.concatenate([tile_in(inp["ev_w_out"][i]) for i in range(NE)], axis=1)
    if NO:
        sh["owi"] = np.concatenate([tile_in(inp["od_w_in"][i]) for i in range(NO)], axis=1)
        sh["owo"] = np.concatenate([tile_in(inp["od_w_out"][i]) for i in range(NO)], axis=1)
        sh["opw"] = np.concatenate(
            [np.ascontiguousarray(inp["od_pool_w"][i].reshape(4, 2, P, 256).transpose(2, 0, 1, 3)).reshape(P, -1)
             for i in range(NO)], axis=1)
    ident, bias, invc = const_tables()
    cols = [vec_cols(inp["norm_g"][:DEPTH]),
            np.tile(inp["ev_q_gain"][:NE].T, (1, 1)).reshape(P, NE),
            np.tile(inp["ev_k_gain"][:NE].T, (1, 1)).reshape(P, NE),
            vec_cols(inp["ev_conv_w"][:NE])]
    if NO:
        cols += [vec_cols(inp["od_conv_w"][:NO]), vec_cols(inp["od_conv_b"][:NO]),
                 vec_cols(inp["od_ln_g"][:NO]), vec_cols(inp["od_ln_b"][:NO]),
                 vec_cols(inp["od_pool_scale"][:NO])]
    else:
        cols += [np.zeros((P, 31 * 8), np.float32)] + [np.zeros((P, 8), np.float32)] * 4
    cols.append(invc)
    sh["cvec"] = np.ascontiguousarray(np.concatenate(cols, axis=1).astype(np.float32))
    sh["ident"] = ident
    sh["biast"] = bias
    return sh


def run(cfg, inp, trace=False):
    nc, NCV, cvo = build_program(cfg)
    sh = prep_shared(cfg, inp)
    assert sh["cvec"].shape[1] == NCV, (sh["cvec"].shape, NCV)
    x = np.asarray(inp["x"])
    in_maps = []
    for c in range(cfg.NCORES):
        m = dict(sh)
        m["xT"] = np.ascontiguousarray(x[c].reshape(cfg.S, DC, P).transpose(2, 1, 0))
        in_maps.append(m)
    res = run_bass_kernel_spmd(nc, in_maps, core_ids=list(range(cfg.NCORES)), trace=trace)
    outs = [np.ascontiguousarray(r["yT"].transpose(2, 1, 0)).reshape(cfg.S, D) for r in res.results]
    return np.stack(outs, axis=0), res


def kernel(**inputs):
    cfg = Cfg()
    inp = {k: np.asarray(v) for k, v in inputs.items()}
    out, _ = run(cfg, inp)
    return out.astype(np.float32)
```

```python
import numpy as np
from contextlib import ExitStack
import concourse.bass as bass
import concourse.mybir as mybir
from concourse.bass_utils import run_bass_kernel_spmd

F32 = mybir.dt.float32
BF16 = mybir.dt.bfloat16
AF = mybir.ActivationFunctionType
ALU = mybir.AluOpType

P = 128
D = 2048
DC = D // P
T = 512
FW = 256
EPS = 1e-6
NEG = -30000.0
CONV_PIECE = 8192


class Cfg:
    def __init__(self, S=4096, FF=5632, DEPTH=4, NCORES=8, only=None):
        self.S, self.FF, self.DEPTH, self.NCORES = S, FF, DEPTH, NCORES
        self.only = only
        self.NT = S // T
        self.NF = FF // P
        self.NE = (DEPTH + 1) // 2
        self.NO = DEPTH // 2


class Buf:
    __slots__ = ("name", "lw", "rd", "rd_dma")

    def __init__(self, name=""):
        self.name = name
        self.lw = None
        self.rd = {}
        self.rd_dma = []


class Op:
    __slots__ = ("eng", "fn", "deps", "sig", "tok", "idx", "dma", "pre")

    def __init__(self, eng, fn, dma):
        self.eng, self.fn, self.dma = eng, fn, dma
        self.deps = None
        self.sig = dma
        self.tok = None
        self.pre = None


COMPUTE = ("pe", "act", "dve")
QUEUES = ("sp", "pool")
SEM_LIMIT = 12000
DMA_RING = 12


class Prog:
    def __init__(self):
        self.ops = {e: [] for e in COMPUTE + QUEUES}
        self.barrier_ops = []

    def add(self, eng, fn, reads=(), writes=(), dma=False):
        op = Op(eng, fn, dma)
        deps = {}

        def dep(o):
            if o is None:
                return
            if o.dma:
                deps[id(o)] = o
            else:
                if o.eng == "pe" and eng == "pe":
                    return
                k = o.eng
                cur = deps.get(k)
                if cur is None or cur.idx < o.idx:
                    deps[k] = o

        for b in reads:
            dep(b.lw)
        for b in writes:
            dep(b.lw)
            for o in b.rd.values():
                dep(o)
            for o in b.rd_dma:
                dep(o)
        for o in self.barrier_ops:
            dep(o)
        op.idx = len(self.ops[eng])
        self.ops[eng].append(op)
        op.deps = list(deps.values())
        for o in op.deps:
            o.sig = True
        for b in reads:
            if dma:
                b.rd_dma.append(op)
            else:
                b.rd[eng] = op
        for b in writes:
            b.lw = op
            b.rd = {}
            b.rd_dma = []
        return op

    def barrier(self):
        bo = []
        for e in COMPUTE:
            if self.ops[e]:
                o = self.ops[e][-1]
                o.sig = True
                bo.append(o)
        for q in QUEUES:
            for o in self.ops[q][-DMA_RING:]:
                bo.append(o)
        self.barrier_ops = bo

    def emit(self, nc, es):
        sems = {}
        n_sem = [0]

        def new_sem(tag):
            n_sem[0] += 1
            return es.enter_context(nc.semaphore(f"{tag}{n_sem[0]}"))

        for e in COMPUTE:
            cur = new_sem(e)
            cnt = 0
            for op in self.ops[e]:
                if op.sig:
                    if cnt >= SEM_LIMIT:
                        cur = new_sem(e)
                        cnt = 0
                    cnt += 1
                    op.tok = (cur, cnt, 1)
        for q in QUEUES:
            ring = [[new_sem(q), 0] for _ in range(DMA_RING)]
            for k, op in enumerate(self.ops[q]):
                slot = ring[k % DMA_RING]
                if slot[1] >= SEM_LIMIT:
                    slot[0] = new_sem(q)
                    slot[1] = 0
                op.pre = (slot[0], slot[1]) if slot[1] > 0 else None
                slot[1] += 16
                op.tok = (slot[0], slot[1], 16)
        self.n_sems = n_sem[0]
        self.stats = {e: (len(v), sum(1 for o in v if o.sig)) for e, v in self.ops.items()}

        with nc.Block() as block:
            def run(eng_name, handle):
                waited = {}

                def wait(sem, val):
                    k = id(sem)
                    if waited.get(k, 0) >= val:
                        return
                    waited[k] = val
                    handle.wait_ge(sem, val)

                for op in self.ops[eng_name]:
                    if op.pre is not None:
                        wait(*op.pre)
                    for o in op.deps:
                        wait(o.tok[0], o.tok[1])
                    ins = op.fn(handle)
                    if op.sig:
                        ins.then_inc(op.tok[0], op.tok[2])
                if eng_name in QUEUES:
                    for op in self.ops[eng_name][-DMA_RING:]:
                        wait(op.tok[0], op.tok[1])

            @block.tensor
            def _(h):
                run("pe", h)

            @block.scalar
            def _(h):
                run("act", h)

            @block.vector
            def _(h):
                run("dve", h)

            @block.sync
            def _(h):
                run("sp", h)

            @block.gpsimd
            def _(h):
                run("pool", h)


def tile_in(w):
    K, N = w.shape
    return np.ascontiguousarray(w.reshape(K // P, P, N // FW, FW).transpose(1, 2, 0, 3)).reshape(P, -1)


def tile_w2(w):
    FFn, N = w.shape
    return np.ascontiguousarray(w.reshape(FFn // P, P, N // P, P).transpose(1, 2, 0, 3)).reshape(P, -1)


def vec_cols(v):
    lead = v.shape[:-1]
    n = v.shape[-1] // P
    a = v.reshape(*lead, n, P)
    a = np.moveaxis(a, -1, 0)
    return np.ascontiguousarray(a).reshape(P, -1)


def alibi_exps():
    return list(range(-8, 4))


def const_tables():
    ident = np.eye(P, dtype=np.float32)
    j = np.arange(P)[:, None].astype(np.float32)
    i = np.arange(P)[None, :].astype(np.float32)
    tabs = []
    for e in alibi_exps():
        sl = np.float32(2.0 ** e)
        prev = np.where(j >= i, -sl * (i + P - j), NEG).astype(np.float32)
        cur = np.where(j <= i, -sl * (i - j), NEG).astype(np.float32)
        tabs.append(np.concatenate([prev, cur], axis=1))
    bias = np.concatenate(tabs, axis=1)
    invc = np.tile((1.0 / np.arange(1, 17, dtype=np.float32))[None, :], (P, 1))
    return ident, bias.astype(np.float32), invc.astype(np.float32)


def build_program(cfg):
    S, FF, DEPTH, NT, NF, NE, NO = cfg.S, cfg.FF, cfg.DEPTH, cfg.NT, cfg.NF, cfg.NE, cfg.NO
    NJF = FF // FW
    nc = bass.Bass("TRN2", target_bir_lowering=False)
    pr = Prog()
    es = ExitStack()

    def din(name, shape, dt=F32):
        return nc.dram_tensor(name, list(shape), dt, kind="ExternalInput")

    def dscr(name, shape, dt):
        return nc.dram_tensor(name, list(shape), dt)

    K13 = 16 * FF
    K2 = 16 * FF
    KEI = 16 * 12288
    KO = 16 * 2048
    KOI = 16 * 3072
    KPW = 4 * 2 * 256
    xin = din("xT", [P, DC, S])
    yout = nc.dram_tensor("yT", [P, DC, S], F32, kind="ExternalOutput")
    wsrc = {
        "w1": din("w1", [P, DEPTH * 2 * K13]),
        "w3": din("w3", [P, DEPTH * 2 * K13]),
        "w2": din("w2", [P, DEPTH * 2 * K2]),
        "ewi": din("ewi", [P, NE * KEI]),
        "ewo": din("ewo", [P, NE * KO]),
    }
    if NO:
        wsrc["owi"] = din("owi", [P, NO * KOI])
        wsrc["owo"] = din("owo", [P, NO * KO])
        wsrc["opw"] = din("opw", [P, NO * KPW])
    wbf = {k: dscr(k + "_bf", v.shape, BF16) for k, v in wsrc.items()}
    wbuf = {k: [Buf(f"{k}{i}") for i in range((v.shape[1] + CONV_PIECE - 1) // CONV_PIECE)]
            for k, v in wsrc.items()}

    cvo = {}
    ncv = 0

    def cv_alloc(name, n):
        nonlocal ncv
        cvo[name] = ncv
        ncv += n

    cv_alloc("ng", DEPTH * 3 * DC)
    cv_alloc("qg", NE)
    cv_alloc("kg", NE)
    cv_alloc("ecw", NE * 3 * 8)
    cv_alloc("ocw", max(NO, 1) * 31 * 8)
    cv_alloc("ocb", max(NO, 1) * 8)
    cv_alloc("lng", max(NO, 1) * 8)
    cv_alloc("lnb", max(NO, 1) * 8)
    cv_alloc("psc", max(NO, 1) * 8)
    cv_alloc("invc", 16)
    NCV = ncv
    cvin = din("cvec", [P, NCV])
    identin = din("ident", [P, P])
    biasin = din("biast", [P, 12 * 256])

    QKT = dscr("qkT", [48, P, S], BF16)
    VT = dscr("vT", [24, P, S], BF16)
    H32 = dscr("h32", [24, P, S], F32)
    CATT = dscr("catT", [P, DC, S], BF16)
    U32 = dscr("u32", [P, 8, S], F32)
    PLT = dscr("plT", [P, 8, S], BF16)

    def sb(name, shape, dt):
        return es.enter_context(nc.sbuf_tensor(name, list(shape), dt))

    RA = sb("RA", [P, 8192], F32)
    RB = sb("RB", [P, 8192], BF16)
    RC = sb("RC", [P, 22528], BF16)
    WR = [sb(f"WR{i}", [P, 16 * FW], BF16) for i in range(4)]
    W2R = [sb(f"W2R{i}", [P, NF * P], BF16) for i in range(2)]
    TMP = [sb(f"TMP{i}", [P, T], F32) for i in range(7)]
    LNM = sb("LNM", [P, T], F32)
    LNR = sb("LNR", [P, T], F32)
    bLNM, bLNR = Buf("lnm"), Buf("lnr")
    OUTB = [sb(f"OUTB{i}", [P, T], F32) for i in range(4)]
    XCB = [sb(f"XCB{i}", [P, T], F32) for i in range(2)]
    PTB = [sb(f"PTB{i}", [P, 512], BF16) for i in range(4)]
    CV = sb("CV", [P, NCV], F32)
    QGS = sb("QGS", [P, max(NE, 1)], F32)
    IDENT = sb("IDENT", [P, P], BF16)
    ONES16 = sb("ONES16", [P, P], BF16)
    ONES32 = sb("ONES32", [P, P], F32)
    BIAST = sb("BIAST", [P, 12 * 256], BF16)
    POOLW = sb("POOLW", [P, KPW], BF16)

    PS = [es.enter_context(nc.psum_tensor(f"PS{i}", [P, T], F32)) for i in range(8)]
    bPS = [Buf(f"ps{i}") for i in range(8)]

    bRA, bRB, bRC = Buf("RA"), Buf("RB"), Buf("RC")
    bWR = [Buf(f"wr{i}") for i in range(4)]
    bW2R = [Buf(f"w2r{i}") for i in range(2)]
    bTMP = [Buf(f"tmp{i}") for i in range(7)]
    bOUT = [Buf(f"out{i}") for i in range(4)]
    bXC = [Buf(f"xc{i}") for i in range(2)]
    bPT = [Buf(f"pt{i}") for i in range(4)]
    bCONST = Buf("const")
    bPOOLW = Buf("poolw")
    bXIN = [[Buf() for _ in range(DC)] for _ in range(NT)]
    bY = [[Buf() for _ in range(DC)] for _ in range(NT)]
    bQK = [[Buf() for _ in range(NT)] for _ in range(48)]
    bV = [[Buf() for _ in range(NT)] for _ in range(24)]
    bH = [[Buf() for _ in range(NT)] for _ in range(24)]
    bCAT = [[Buf() for _ in range(NT)] for _ in range(DC)]
    bU = [[Buf() for _ in range(NT)] for _ in range(8)]
    bPL = [[Buf() for _ in range(NT)] for _ in range(8)]

    rot = {"tmp": 0, "out": 0, "xc": 0, "wr": 0, "w2r": 0, "pt": 0}

    def nxt(kind, n):
        v = rot[kind]
        rot[kind] = (v + 1) % n
        return v

    def cvc(name, idx):
        o = cvo[name] + idx
        return CV[:, o:o + 1]

    def dma(q, out, in_, reads, writes):
        return pr.add(q, lambda h: h.dma_start(out=out, in_=in_), reads, writes, dma=True)

    conv_pending = []
    conv_seen = set()
    conv_state = {"every": 1, "cnt": 0, "defer": False, "gate": False}

    def emit_conv(kind, pi):
        tot = wsrc[kind].shape[1]
        a = pi * CONV_PIECE
        b = min(tot, a + CONV_PIECE)
        dma("pool", wbf[kind][:, a:b], wsrc[kind][:, a:b], [], [wbuf[kind][pi]])

    def convert(kind, lo, hi):
        p0 = lo // CONV_PIECE
        p1 = (hi + CONV_PIECE - 1) // CONV_PIECE
        for pi in range(p0, p1):
            if (kind, pi) in conv_seen:
                continue
            conv_seen.add((kind, pi))
            if conv_state["defer"]:
                conv_pending.append((kind, pi))
            else:
                emit_conv(kind, pi)

    def store(out, in_, reads, writes):
        dma("pool", out, in_, reads, writes)
        conv_state["cnt"] += 1
        if conv_state["gate"] and conv_pending and conv_state["cnt"] % conv_state["every"] == 0:
            emit_conv(*conv_pending.pop(0))

    def conv_gate_on():
        conv_state["gate"] = True
        conv_state["cnt"] = 0

    def conv_flush():
        conv_state["gate"] = False
        while conv_pending:
            emit_conv(*conv_pending.pop(0))

    def wread(kind, off, n):
        return wbuf[kind][off // CONV_PIECE:(off + n - 1) // CONV_PIECE + 1]

    def load_w(kind, off, n, dst_ap, dst_buf):
        dma("sp", dst_ap, wbf[kind][:, off:off + n], wread(kind, off, n), [dst_buf])

    def act(out, in_, func, reads, writes, bias=None, scale=None):
        kw = {}
        if bias is not None:
            kw["bias"] = bias
        if scale is not None:
            kw["scale"] = scale
        return pr.add("act", lambda h: h.activation(out=out, in_=in_, func=func, **kw), reads, writes)

    def stt(out, in0, scalar, in1, op0, op1, reads, writes):
        return pr.add("dve", lambda h: h.scalar_tensor_tensor(out=out, in0=in0, scalar=scalar, in1=in1,
                                                              op0=op0, op1=op1), reads, writes)

    def tt(out, in0, in1, op, reads, writes):
        return pr.add("dve", lambda h: h.tensor_tensor(out=out, in0=in0, in1=in1, op=op), reads, writes)

    def tsc(out, in0, s1, s2, op0, op1, reads, writes):
        if op1 is None:
            return pr.add("dve", lambda h: h.tensor_scalar(out=out, in0=in0, scalar1=s1, scalar2=None,
                                                           op0=op0), reads, writes)
        return pr.add("dve", lambda h: h.tensor_scalar(out=out, in0=in0, scalar1=s1, scalar2=s2,
                                                       op0=op0, op1=op1), reads, writes)

    def recip(ap, buf):
        return pr.add("dve", lambda h: h.reciprocal(out=ap, in_=ap), [buf], [buf])

    def mm_group(out, pairs, reads, writes):
        n = len(pairs)

        def fn(h):
            ins = None
            for i, (l, r) in enumerate(pairs):
                ins = h.matmul(out, l, r, start=(i == 0), stop=(i == n - 1))
            return ins
        return pr.add("pe", fn, reads, writes)

    pr.add("dve", lambda h: h.memset(ONES32[:], 1.0), [], [bCONST])
    pr.add("dve", lambda h: h.memset(ONES16[:], 1.0), [], [bCONST])
    dma("sp", CV[:], cvin[:, :], [], [bCONST])
    dma("pool", IDENT[:], identin[:, :], [], [bCONST])
    dma("pool", BIAST[:], biasin[:, :], [], [bCONST])
    if NE:
        tsc(QGS[:, 0:NE], CV[:, cvo["qg"]:cvo["qg"] + NE], float(P ** -0.5), None, ALU.mult, None,
            [bCONST], [bCONST])

    XA = RA[:].rearrange("p (c t) -> p c t", c=DC)
    XN = RB[:].rearrange("p (c t) -> p c t", c=DC)
    bXNc = [Buf(f"xn{c}") for c in range(DC)]

    def x_src(first):
        return (xin, bXIN) if first else (yout, bY)

    XN2 = RC[:, 0:DC * T].rearrange("p (c t) -> p c t", c=DC)
    bXN2c = [Buf(f"xnb{c}") for c in range(DC)]
    RSTD = sb("RSTD", [P, T], F32)
    bRSTD = Buf("rstd")
    SQB = [sb(f"SQB{i}", [P, T], BF16) for i in range(8)]
    bSQB = [Buf(f"sqb{i}") for i in range(8)]
    sq_state = {"c": DC, "mm": DC}

    def prologue_begin(i, src, srcb):
        t0 = i * T
        dma("sp", XA, src[:, :, t0:t0 + T], srcb[i], [bRA])
        sq_state["c"] = 0
        sq_state["mm"] = 0

    def prologue_step():
        while sq_state["mm"] < sq_state["c"]:
            c = sq_state["mm"]
            pr.add("pe", (lambda c=c: lambda h: h.matmul(PS[6][:], ONES16[:], SQB[c % 8][:],
                                                         start=(c == 0), stop=(c == DC - 1)))(),
                   [bSQB[c % 8], bCONST], [bPS[6]])
            sq_state["mm"] += 1
        n = 0
        while sq_state["c"] < DC and n < 8:
            c = sq_state["c"]
            act(SQB[c % 8][:], XA[:, c, :], AF.Square, [bRA], [bSQB[c % 8]])
            sq_state["c"] += 1
            n += 1

    def prologue_busy():
        return sq_state["mm"] < DC

    def prologue_flush():
        while prologue_busy():
            prologue_step()

    def prologue_norm(gidx, xn_view, xn_bufs):
        act(RSTD[:], PS[6][:], AF.Sqrt, [bPS[6]], [bRSTD], bias=EPSB[:, 0:1], scale=1.0 / D)
        recip(RSTD[:], bRSTD)
        for c in range(DC):
            stt(xn_view[:, c, :], XA[:, c, :], cvc("ng", gidx * DC + c), RSTD[:], ALU.mult, ALU.mult,
                [bRA, bRSTD, bCONST], [xn_bufs[c]])

    EPSB = sb("EPSB", [P, 1], F32)
    pr.add("dve", lambda h: h.memset(EPSB[:], EPS), [], [bCONST])
    pr.barrier()

    G = RC[:, 0:NF * T].rearrange("p (f t) -> p f t", f=NF)
    bGc = [Buf(f"g{f}") for f in range(NF)]

    def ffn(l, s, first):
        src, srcb = x_src(first)
        gidx = l * 3 + (0 if s == 0 else 2)
        base13 = (l * 2 + s) * K13
        base2 = (l * 2 + s) * K2
        jm = NJF // 2
        prologue_begin(0, src, srcb)
        prologue_flush()
        prologue_norm(gidx, XN, bXNc)
        for i in range(NT):
            t0 = i * T
            for j in range(NJF):
                if i + 1 < NT and j == jm:
                    prologue_begin(i + 1, src, srcb)
                if i + 1 < NT and j >= jm and prologue_busy():
                    prologue_step()
                k1 = nxt("wr", 4)
                load_w("w1", base13 + j * 16 * FW, 16 * FW, WR[k1][:], bWR[k1])
                k3 = nxt("wr", 4)
                load_w("w3", base13 + j * 16 * FW, 16 * FW, WR[k3][:], bWR[k3])
                w1v = WR[k1][:].rearrange("p (k f) -> p k f", k=16)
                w3v = WR[k3][:].rearrange("p (k f) -> p k f", k=16)
                for m in range(FW // P):
                    fc = j * (FW // P) + m
                    b1 = fc % 2
                    b3 = 2 + fc % 2
                    mm_group(PS[b1][:], [(w1v[:, kc, m * P:(m + 1) * P], XN[:, kc, :]) for kc in range(DC)],
                             [bWR[k1]] + bXNc, [bPS[b1]])
                    mm_group(PS[b3][:], [(w3v[:, kc, m * P:(m + 1) * P], XN[:, kc, :]) for kc in range(DC)],
                             [bWR[k3]] + bXNc, [bPS[b3]])
                    kt = nxt("tmp", 7)
                    act(TMP[kt][:], PS[b1][:], AF.Silu, [bPS[b1]], [bTMP[kt]])
                    tt(G[:, fc, :], PS[b3][:], TMP[kt][:], ALU.mult, [bPS[b3], bTMP[kt]], [bGc[fc]])
            if i + 1 < NT:
                prologue_flush()
                prologue_norm(gidx, XN, bXNc)
            for dc in range(DC):
                k2 = nxt("w2r", 2)
                load_w("w2", base2 + dc * NF * P, NF * P, W2R[k2][:], bW2R[k2])
                w2v = W2R[k2][:].rearrange("p (f d) -> p f d", f=NF)
                kx = nxt("xc", 2)
                dma("sp", XCB[kx][:], src[:, dc, t0:t0 + T], [srcb[i][dc]], [bXC[kx]])
                yb = 4 + dc % 2
                mm_group(PS[yb][:], [(w2v[:, fc, :], G[:, fc, :]) for fc in range(NF)],
                         [bW2R[k2]] + bGc, [bPS[yb]])
                ko = nxt("out", 4)
                stt(OUTB[ko][:], PS[yb][:], 0.5, XCB[kx][:], ALU.mult, ALU.add,
                    [bPS[yb], bXC[kx]], [bOUT[ko]])
                store(yout[:, dc, t0:t0 + T], OUTB[ko][:], [bOUT[ko]], [bY[i][dc]])

    def projection(l, kind, base, ncol_chunks, chunk_kind, gain_ap):
        gidx = l * 3 + 1
        per_tile = FW // P
        xnb = [(XN, bXNc), (XN2, bXN2c)]
        NJ = ncol_chunks // per_tile
        jm = NJ // 2
        prologue_begin(0, yout, bY)
        prologue_flush()
        prologue_norm(gidx, *xnb[0])
        for i in range(NT):
            t0 = i * T
            XNi, bXNi = xnb[i % 2]
            pend = []

            def qk_tail(t0=t0):
                for (cc_, pb_, ck_, kt_, sq16_) in pend:
                    sb_ = 6 if False else 7
                    pr.add("pe", (lambda sq16_=sq16_, sb_=sb_: lambda h: h.matmul(
                        PS[sb_][:], ONES16[:], sq16_, start=True, stop=True))(),
                        [bTMP[kt_], bCONST], [bPS[sb_]])
                    kr = nxt("tmp", 7)
                    act(TMP[kr][:], PS[sb_][:], AF.Sqrt, [bPS[sb_]], [bTMP[kr]],
                        bias=EPSB[:, 0:1], scale=1.0 / P)
                    recip(TMP[kr][:], bTMP[kr])
                    ko_ = nxt("out", 4)
                    o16 = OUTB[ko_][:].bitcast(BF16)[:, 0:T]
                    stt(o16, PS[pb_][:], ck_[2], TMP[kr][:], ALU.mult, ALU.mult,
                        [bPS[pb_], bTMP[kr], bCONST], [bOUT[ko_]])
                    store(QKT[ck_[1], :, t0:t0 + T], o16, [bOUT[ko_]], [bQK[ck_[1]][i]])
                del pend[:]

            normed = [False]
            if kind == "ewi":
                half = NJ // 2
                order = [x for pair in zip(range(half), range(half, NJ)) for x in pair]
            else:
                order = list(range(NJ))
            pcnt = 0
            for jpos, j in enumerate(order):
                if i + 1 < NT and jpos == jm:
                    prologue_begin(i + 1, yout, bY)
                if i + 1 < NT and jpos >= jm and prologue_busy():
                    prologue_step()
                    if not prologue_busy():
                        prologue_norm(gidx, *xnb[(i + 1) % 2])
                        normed[0] = True
                kw = nxt("wr", 4)
                load_w(kind, base + j * 16 * FW, 16 * FW, WR[kw][:], bWR[kw])
                wv = WR[kw][:].rearrange("p (k f) -> p k f", k=16)
                for m in range(per_tile):
                    cc = j * per_tile + m
                    pb = pcnt % 4
                    pcnt += 1
                    mm_group(PS[pb][:], [(wv[:, kc, m * P:(m + 1) * P], XNi[:, kc, :]) for kc in range(DC)],
                             [bWR[kw]] + bXNi, [bPS[pb]])
                    ck = chunk_kind(cc)
                    if ck[0] == "qk":
                        kt = nxt("tmp", 7)
                        sq16 = TMP[kt][:].bitcast(BF16)[:, 0:T]
                        act(sq16, PS[pb][:], AF.Square, [bPS[pb]], [bTMP[kt]])
                        qk_tail()
                        pend.append((cc, pb, ck, kt, sq16))
                    else:
                        qk_tail()
                        ko = nxt("out", 4)
                        if ck[0] == "v":
                            o16 = OUTB[ko][:].bitcast(BF16)[:, 0:T]
                            act(o16, PS[pb][:], AF.Copy, [bPS[pb]], [bOUT[ko]])
                            store(VT[ck[1], :, t0:t0 + T], o16, [bOUT[ko]], [bV[ck[1]][i]])
                        else:
                            act(OUTB[ko][:], PS[pb][:], AF.Copy, [bPS[pb]], [bOUT[ko]])
                            store(H32[ck[1], :, t0:t0 + T], OUTB[ko][:], [bOUT[ko]], [bH[ck[1]][i]])
            qk_tail()
            if i + 1 < NT and not normed[0]:
                prologue_flush()
                prologue_norm(gidx, *xnb[(i + 1) % 2])

    CAT = RB[:].rearrange("p (c t) -> p c t", c=DC)
    bCATc = [Buf(f"cat{c}") for c in range(DC)]
    CAT2 = RC[:, 0:DC * T].rearrange("p (c t) -> p c t", c=DC)
    bCAT2c = [Buf(f"catb{c}") for c in range(DC)]
    CATS = [(CAT, bCATc), (CAT2, bCAT2c)]

    def out_proj(kind, base, i, CATv=None, bCATv=None):
        CATv = CAT if CATv is None else CATv
        bCATv = bCATc if bCATv is None else bCATv
        t0 = i * T
        per_tile = FW // P
        for j in range(DC // per_tile):
            kw = nxt("wr", 4)
            load_w(kind, base + j * 16 * FW, 16 * FW, WR[kw][:], bWR[kw])
            wv = WR[kw][:].rearrange("p (k f) -> p k f", k=16)
            for m in range(per_tile):
                dc = j * per_tile + m
                pb = dc % 4
                kx = nxt("xc", 2)
                dma("sp", XCB[kx][:], yout[:, dc, t0:t0 + T], [bY[i][dc]], [bXC[kx]])
                mm_group(PS[pb][:], [(wv[:, kc, m * P:(m + 1) * P], CATv[:, kc, :]) for kc in range(DC)],
                         [bWR[kw]] + bCATv, [bPS[pb]])
                ko = nxt("out", 4)
                tt(OUTB[ko][:], PS[pb][:], XCB[kx][:], ALU.add, [bPS[pb], bXC[kx]], [bOUT[ko]])
                store(yout[:, dc, t0:t0 + T], OUTB[ko][:], [bOUT[ko]], [bY[i][dc]])

    def even_mixer(l):
        ie = l // 2
        qg = QGS[:, ie:ie + 1]
        kg = cvc("kg", ie)

        def ck(cc):
            if cc < 24:
                return ("qk", cc, qg)
            if cc < 48:
                return ("qk", cc, kg)
            if cc < 72:
                return ("v", cc - 48)
            return ("h", cc - 72)
        projection(l, "ewi", ie * KEI, 96, ck, None)
        pr.barrier()
        conv_gate_on()
        ACO = RA[:, 0:S]
        ACD = RA[:, S:2 * S]
        bACO, bACD = Buf("aco"), Buf("acd")
        qkv = []
        for b in range(2):
            if b == 0:
                qkv.append((RB[:, 0:S], RB[:, S:2 * S], RC[:, 0:S]))
            else:
                qkv.append((RC[:, S:2 * S], RC[:, 2 * S:3 * S], RC[:, 3 * S:4 * S]))
        bq = [[Buf(), Buf(), Buf()] for _ in range(2)]
        NBLK = S // P
        VBLK = RC[:, 4 * S:5 * S].rearrange("p (b d) -> p b d", b=NBLK)
        bVB = Buf("vblk")
        YA = RC[:, 5 * S:5 * S + 4 * T].rearrange("p (k t) -> p k t", k=4)
        bYA = [Buf() for _ in range(4)]
        ps7b = PS[7][:].bitcast(BF16)
        ps6b = PS[6][:].bitcast(BF16)
        SBK = [0, 1, 6]
        def attn_hg(h, g, qb):
            d = 4 ** g
            e_idx = (2 * g - (h + 1)) + 8
            bprev = BIAST[:, e_idx * 256:e_idx * 256 + 128]
            bcur = BIAST[:, e_idx * 256 + 128:e_idx * 256 + 256]
            QTv, KTv, VTv = qkv[qb]
            hq, hk = g * 8 + h, 24 + g * 8 + h
            dma("sp", QTv, QKT[hq, :, :], bQK[hq], [bq[qb][0]])
            dma("sp", KTv, QKT[hk, :, :], bQK[hk], [bq[qb][1]])
            dma("sp", VTv, VT[g * 8 + h, :, :], bV[g * 8 + h], [bq[qb][2]])
            NB = S // (P * d)
            blocks = [(n, r) for n in range(NB) for r in range(d)]

            def tok(n, r):
                st_ = P * n * d + r
                return slice(st_, st_ + (P - 1) * d + 1, d)

            for rnd, b0 in enumerate(range(0, NBLK, 8)):
                tb = 7 if rnd % 2 == 0 else 6
                tview = ps7b if tb == 7 else ps6b

                def tfn(h_, b0=b0, tview=tview):
                    ins = None
                    for bi in range(8):
                        n, r = blocks[b0 + bi]
                        ins = h_.transpose(tview[:, bi * P:(bi + 1) * P], VTv[:, tok(n, r)], IDENT[:])
                    return ins
                pr.add("pe", tfn, [bq[qb][2], bCONST], [bPS[tb]])
                pr.add("act", (lambda b0=b0, tview=tview: lambda h_: h_.activation(
                    out=VBLK[:, b0:b0 + 8, :], in_=tview[:, 0:8 * P].rearrange("p (b d) -> p b d", b=8),
                    func=AF.Copy))(), [bPS[tb]], [bVB])

            def scores(sbi):
                bank = SBK[sbi % 3]

                def fn(h_):
                    ins = None
                    for k in range(2):
                        n, r = blocks[2 * sbi + k]
                        off = k * 256
                        qs = QTv[:, tok(n, r)]
                        h_.matmul(PS[bank][:, off + 128:off + 256], KTv[:, tok(n, r)], qs, start=True, stop=False)
                        ins = h_.matmul(PS[bank][:, off + 128:off + 256], IDENT[:], bcur, start=False, stop=True)
                        if n > 0:
                            h_.matmul(PS[bank][:, off:off + 128], KTv[:, tok(n - 1, r)], qs, start=True, stop=False)
                            ins = h_.matmul(PS[bank][:, off:off + 128], IDENT[:], bprev, start=False, stop=True)
                    return ins
                pr.add("pe", fn, [bq[qb][0], bq[qb][1], bCONST], [bPS[bank]])

            def expo(sbi):
                bank = SBK[sbi % 3]
                kp = sbi % 4
                n0 = blocks[2 * sbi][0]
                n1 = blocks[2 * sbi + 1][0]
                if n0 > 0 and n1 > 0:
                    act(PTB[kp][:, 0:512], PS[bank][:, 0:512], AF.Exp, [bPS[bank]], [bPT[kp]])
                else:
                    for k, n in ((0, n0), (1, n1)):
                        lo = k * 256 + (0 if n > 0 else 128)
                        hi = k * 256 + 256
                        act(PTB[kp][:, lo:hi], PS[bank][:, lo:hi], AF.Exp, [bPS[bank]], [bPT[kp]])

            def pv(sbi):
                kp = sbi % 4
                grp = sbi // 2
                ob = 2 + grp % 2
                db = 4 + grp % 2

                def fn(h_):
                    ins = None
                    for k in range(2):
                        bi = 2 * sbi + k
                        n, r = blocks[bi]
                        col = (bi % 4) * P
                        off = k * 256
                        h_.matmul(PS[ob][:, col:col + P], VBLK[:, bi, :], PTB[kp][:, off + 128:off + 256],
                                  start=True, stop=(n == 0))
                        if n > 0:
                            h_.matmul(PS[ob][:, col:col + P], VBLK[:, bi - d, :], PTB[kp][:, off:off + 128],
                                      start=False, stop=True)
                        ins = h_.matmul(PS[db][:, col:col + P], ONES16[:], PTB[kp][:, off + 128:off + 256],
                                        start=True, stop=(n == 0))
                        if n > 0:
                            ins = h_.matmul(PS[db][:, col:col + P], ONES16[:], PTB[kp][:, off:off + 128],
                                            start=False, stop=True)
                    return ins
                pr.add("pe", fn, [bPT[kp], bVB, bCONST], [bPS[ob], bPS[db]])

            def accum(grp):
                n, r0 = blocks[grp * 4]
                ob = 2 + grp % 2
                db = 4 + grp % 2
                if d == 1:
                    sl = slice(grp * 4 * P, grp * 4 * P + 4 * P)
                    dsto, dstd = ACO[:, sl], ACD[:, sl]
                    srco, srcd = PS[ob][:], PS[db][:]
                elif d == 4:
                    sl = slice(n * 512, n * 512 + 512)
                    dsto = ACO[:, sl].rearrange("p (i r) -> p r i", r=4)
                    dstd = ACD[:, sl].rearrange("p (i r) -> p r i", r=4)
                    srco = PS[ob][:].rearrange("p (r i) -> p r i", r=4)
                    srcd = PS[db][:].rearrange("p (r i) -> p r i", r=4)
                else:
                    sl = slice(n * 2048, n * 2048 + 2048)
                    dsto = ACO[:, sl].rearrange("p (i r) -> p r i", r=16)[:, r0:r0 + 4, :]
                    dstd = ACD[:, sl].rearrange("p (i r) -> p r i", r=16)[:, r0:r0 + 4, :]
                    srco = PS[ob][:].rearrange("p (r i) -> p r i", r=4)
                    srcd = PS[db][:].rearrange("p (r i) -> p r i", r=4)
                if g == 0:
                    pr.add("act", lambda h_: h_.activation(out=dsto, in_=srco, func=AF.Copy), [bPS[ob]], [bACO])
                    pr.add("act", lambda h_: h_.activation(out=dstd, in_=srcd, func=AF.Copy), [bPS[db]], [bACD])
                else:
                    tt(dsto, srco, dsto, ALU.add, [bPS[ob], bACO], [bACO])
                    tt(dstd, srcd, dstd, ALU.add, [bPS[db], bACD], [bACD])

            NSB = NBLK // 2
            scores(0)
            scores(1)
            for sbi in range(NSB):
                if sbi + 2 < NSB:
                    scores(sbi + 2)
                expo(sbi)
                pv(sbi)
                if sbi % 2 == 1:
                    accum(sbi // 2)

        it = 0
        for h in range(8):
            for g in range(3):
                attn_hg(h, g, it % 2)
                it += 1
            for i in range(NT):
                sl = slice(i * T, (i + 1) * T)
                recip(ACD[:, sl], bACD)
                ky = i % 4
                tt(YA[:, ky, :], ACO[:, sl], ACD[:, sl], ALU.mult, [bACO, bACD], [bYA[ky]])
                store(CATT[:, h, sl], YA[:, ky, :], [bYA[ky]], [bCAT[h][i]])
        pr.barrier()
        RCf = RC[:].bitcast(F32)
        CG32 = [RA[:, 0:S], RCf[:, 0:S]]
        XT32 = [RA[:, S:2 * S], RCf[:, S:2 * S]]
        UB = [RB[:, 0:S + 32], RC[:, 4 * S:4 * S + S + 32]]
        D3 = [WR[0][:, 0:3 * P].rearrange("p (j m) -> p j m", j=3),
              WR[1][:, 0:3 * P].rearrange("p (j m) -> p j m", j=3)]
        bCG, bXT, bUBb, bD = [Buf(), Buf()], [Buf(), Buf()], [Buf(), Buf()], [Buf(), Buf()]
        for s_ in range(2):
            pr.add("dve", (lambda s_=s_: lambda h_: h_.memset(UB[s_][:, 0:32], 0.0))(), [], [bUBb[s_]])
        def e3_prep(c):
            s_ = c % 2
            dma("sp", CG32[s_], H32[8 + c, :, :], bH[8 + c], [bCG[s_]])
            dma("sp", XT32[s_], H32[16 + c, :, :], bH[16 + c], [bXT[s_]])
            tt(UB[s_][:, 32:32 + S], CG32[s_], XT32[s_], ALU.mult, [bCG[s_], bXT[s_]], [bUBb[s_]])
            for tp in range(3):
                tsc(D3[s_][:, tp, :], IDENT[:], cvc("ecw", (ie * 3 + tp) * 8 + c), None, ALU.mult, None,
                    [bCONST], [bD[s_]])

        e3_prep(0)
        for c in range(8):
            s_ = c % 2
            if c + 1 < 8:
                e3_prep(c + 1)
            for i in range(NT):
                t0 = i * T
                pb = i % 4
                kx = nxt("xc", 2)
                dma("sp", XCB[kx][:], H32[c, :, t0:t0 + T], [bH[c][i]], [bXC[kx]])
                mm_group(PS[pb][:], [(D3[s_][:, tp, :], UB[s_][:, 32 + t0 - (2 - tp):32 + t0 - (2 - tp) + T])
                                     for tp in range(3)], [bD[s_], bUBb[s_]], [bPS[pb]])
                ko = nxt("out", 4)
                o16 = OUTB[ko][:].bitcast(BF16)[:, 0:T]
                tt(o16, PS[pb][:], XCB[kx][:], ALU.mult, [bPS[pb], bXC[kx]], [bOUT[ko]])
                store(CATT[:, 8 + c, t0:t0 + T], o16, [bOUT[ko]], [bCAT[8 + c][i]])
        pr.barrier()
        def e4_load(i):
            cv, cb = CATS[i % 2]
            dma("sp", cv, CATT[:, :, i * T:(i + 1) * T], [bCAT[c][i] for c in range(DC)], cb)
        e4_load(0)
        for i in range(NT):
            if i + 1 < NT:
                e4_load(i + 1)
            out_proj("ewo", ie * KO, i, *CATS[i % 2])
        pr.barrier()

    def odd_mixer(l):
        io = l // 2
        projection(l, "owi", io * KOI, 24, lambda cc: ("h", cc), None)
        load_w("opw", io * KPW, KPW, POOLW[:], bPOOLW)
        pr.barrier()
        conv_gate_on()
        RCf = RC[:].bitcast(F32)
        A32 = [RA[:, 0:S], RCf[:, 0:S]]
        G32 = [RA[:, S:2 * S], RCf[:, S:2 * S]]
        UB = [RB[:, 0:S + 32], RC[:, 4 * S:4 * S + S + 32]]
        D31 = [WR[0][:, 0:31 * P].rearrange("p (j m) -> p j m", j=31),
               WR[1][:, 0:31 * P].rearrange("p (j m) -> p j m", j=31)]
        bA, bG, bUBb, bD = [Buf(), Buf()], [Buf(), Buf()], [Buf(), Buf()], [Buf(), Buf()]
        for s_ in range(2):
            pr.add("dve", (lambda s_=s_: lambda h_: h_.memset(UB[s_][:, 0:32], 0.0))(), [], [bUBb[s_]])
        def o2_prep(c):
            s_ = c % 2
            dma("sp", A32[s_], H32[c, :, :], bH[c], [bA[s_]])
            dma("sp", G32[s_], H32[8 + c, :, :], bH[8 + c], [bG[s_]])
            act(G32[s_], G32[s_], AF.Sigmoid, [bG[s_]], [bG[s_]])
            tt(UB[s_][:, 32:32 + S], A32[s_], G32[s_], ALU.mult, [bA[s_], bG[s_]], [bUBb[s_]])
            for jt in range(31):
                tsc(D31[s_][:, jt, :], IDENT[:], cvc("ocw", (io * 31 + jt) * 8 + c), None, ALU.mult, None,
                    [bCONST], [bD[s_]])

        HW_ = T + 16
        W2f = WR[2][:].bitcast(F32)
        W3f = WR[3][:].bitcast(F32)
        ZH = [W2f[:, 0:HW_], W2f[:, HW_:2 * HW_]]
        SAo = W2f[:, 2 * HW_:3 * HW_]
        SBo = W3f[:, 0:HW_]
        tmpc = W3f[:, HW_:HW_ + 16]
        bZH, bSAo, bSBo, bTc = [Buf(), Buf()], Buf(), Buf(), Buf()
        io_ = cvo["invc"]
        zk = [0]

        def o3_tile(c, i):
            g = c // 2
            kwin = 2 ** (g + 1)
            t0 = i * T
            k = zk[0] % 2
            zk[0] += 1
            if i == 0:
                pr.add("dve", (lambda k=k: lambda h_: h_.memset(ZH[k][:, 0:16], 0.0))(), [], [bZH[k]])
                dma("sp", ZH[k][:, 16:HW_], H32[16 + c, :, 0:T], [bH[16 + c][0]], [bZH[k]])
            else:
                dma("sp", ZH[k][:, 0:HW_], H32[16 + c, :, t0 - 16:t0 + T], [bH[16 + c][i - 1], bH[16 + c][i]],
                    [bZH[k]])
            cur, curb = ZH[k], bZH[k]
            pp = [(SAo, bSAo), (SBo, bSBo)]
            for stp in range(g + 1):
                kk = 2 ** stp
                dst, dstb = pp[stp % 2]
                pr.add("dve", (lambda dst=dst, cur=cur, kk=kk: lambda h_: h_.tensor_copy(
                    out=dst[:, 0:kk], in_=cur[:, 0:kk]))(), [curb], [dstb])
                tt(dst[:, kk:HW_], cur[:, kk:HW_], cur[:, 0:HW_ - kk], ALU.add, [curb], [dstb])
                cur, curb = dst, dstb
            ko = nxt("out", 4)
            o16 = OUTB[ko][:].bitcast(BF16)[:, 0:T]
            stt(o16, cur[:, 16:HW_], 1.0 / kwin, ZH[k][:, 16:HW_], ALU.mult, ALU.subtract,
                [curb, bZH[k]], [bOUT[ko]])
            if i == 0:
                nfix = kwin - 1
                tt(tmpc[:, 0:nfix], cur[:, 16:16 + nfix], CV[:, io_:io_ + nfix], ALU.mult, [curb, bCONST], [bTc])
                tt(o16[:, 0:nfix], tmpc[:, 0:nfix], ZH[k][:, 16:16 + nfix], ALU.subtract, [bTc, bZH[k]],
                   [bOUT[ko]])
            store(PLT[:, c, t0:t0 + T], o16, [bOUT[ko]], [bPL[c][i]])

        o2_prep(0)
        for c in range(8):
            s_ = c % 2
            if c + 1 < 8:
                o2_prep(c + 1)
            for i in range(NT):
                t0 = i * T
                pb = i % 4
                mm_group(PS[pb][:], [(D31[s_][:, jt, :], UB[s_][:, 32 + t0 - (30 - jt):32 + t0 - (30 - jt) + T])
                                     for jt in range(31)], [bD[s_], bUBb[s_]], [bPS[pb]])
                ko = nxt("out", 4)
                act(OUTB[ko][:], PS[pb][:], AF.Identity, [bPS[pb], bCONST], [bOUT[ko]],
                    bias=cvc("ocb", io * 8 + c))
                store(U32[:, c, t0:t0 + T], OUTB[ko][:], [bOUT[ko]], [bU[c][i]])
                o3_tile(c, i)
        pr.barrier()
        UT = RA[:, 0:8 * T].rearrange("p (c t) -> p c t", c=8)
        PLt = RA[:, 8 * T:16 * T].bitcast(BF16)[:, 0:8 * T].rearrange("p (c t) -> p c t", c=8)
        bUT, bPLt = Buf(), Buf()
        pwv = POOLW[:].rearrange("p (g c e) -> p g c e", g=4, c=2)

        def o4_ln(i):
            t0 = i * T
            CATv, bCATv = CATS[i % 2]
            dma("sp", UT, U32[:, :, t0:t0 + T], [bU[c][i] for c in range(8)], [bUT])
            dma("sp", PLt, PLT[:, :, t0:t0 + T], [bPL[c][i] for c in range(8)], [bPLt])
            for c in range(8):
                pr.add("pe", (lambda c=c: lambda h_: h_.matmul(PS[6][:], ONES32[:], UT[:, c, :],
                                                               start=(c == 0), stop=(c == 7)))(),
                       [bUT, bCONST], [bPS[6]])
            for c in range(8):
                k = nxt("tmp", 7)
                act(TMP[k][:], UT[:, c, :], AF.Square, [bUT], [bTMP[k]])
                pr.add("pe", (lambda c=c, k=k: lambda h_: h_.matmul(PS[7][:], ONES32[:], TMP[k][:],
                                                                    start=(c == 0), stop=(c == 7)))(),
                       [bTMP[k], bCONST], [bPS[7]])
            act(LNM[:], PS[6][:], AF.Identity, [bPS[6]], [bLNM], scale=1.0 / 1024)
            tt(LNR[:], LNM[:], LNM[:], ALU.mult, [bLNM], [bLNR])
            stt(LNR[:], PS[7][:], 1.0 / 1024, LNR[:], ALU.mult, ALU.subtract,
                [bPS[7], bLNR], [bLNR])
            act(LNR[:], LNR[:], AF.Sqrt, [bLNR], [bLNR], bias=EPSB[:, 0:1], scale=1.0)
            recip(LNR[:], bLNR)
            for c in range(8):
                k1 = nxt("tmp", 7)
                tt(TMP[k1][:], UT[:, c, :], LNM[:], ALU.subtract, [bUT, bLNM], [bTMP[k1]])
                stt(TMP[k1][:], TMP[k1][:], cvc("lng", io * 8 + c), LNR[:], ALU.mult, ALU.mult,
                    [bTMP[k1], bLNR, bCONST], [bTMP[k1]])
                act(CATv[:, c, :], TMP[k1][:], AF.Silu, [bTMP[k1], bCONST], [bCATv[c]],
                    bias=cvc("lnb", io * 8 + c))
            for e in range(8):
                g, ec = e // 2, e % 2
                pb = e % 4
                mm_group(PS[pb][:], [(pwv[:, g, cc, ec * P:(ec + 1) * P], PLt[:, 2 * g + cc, :])
                                     for cc in range(2)], [bPOOLW, bPLt], [bPS[pb]])
                act(CATv[:, 8 + e, :], PS[pb][:], AF.Identity, [bPS[pb], bCONST], [bCATv[8 + e]],
                    scale=cvc("psc", io * 8 + e))

        o4_ln(0)
        for i in range(NT):
            if i + 1 < NT:
                o4_ln(i + 1)
            out_proj("owo", io * KO, i, *CATS[i % 2])
        pr.barrier()

    def conv_ffn(l, s):
        lo, hi = (l * 2 + s) * K13, (l * 2 + s + 1) * K13
        a = lo
        while a < hi:
            b = min(hi, (a // CONV_PIECE + 1) * CONV_PIECE)
            convert("w1", a, b)
            convert("w3", a, b)
            a = b
        convert("w2", (l * 2 + s) * K2, (l * 2 + s + 1) * K2)

    def conv_mixer(l):
        if l % 2 == 0:
            convert("ewi", (l // 2) * KEI, (l // 2 + 1) * KEI)
            convert("ewo", (l // 2) * KO, (l // 2 + 1) * KO)
        else:
            convert("owi", (l // 2) * KOI, (l // 2 + 1) * KOI)
            convert("owo", (l // 2) * KO, (l // 2 + 1) * KO)
            convert("opw", (l // 2) * KPW, (l // 2 + 1) * KPW)

    stages = []
    for l in range(DEPTH):
        stages.append(("ffn", l, 0))
        stages.append(("mix", l, 0))
        stages.append(("ffn", l, 1))

    def conv_stage(st):
        if st[0] == "ffn":
            conv_ffn(st[1], st[2])
        else:
            conv_mixer(st[1])

    if cfg.only is not None:
        stages = [tuple(s) for s in cfg.only]
    def n_stores(st):
        if st[0] == "ffn":
            return DC * NT
        if st[1] % 2 == 0:
            return 96 * NT
        return 24 * NT

    conv_stage(stages[0])
    for si, st in enumerate(stages):
        if si + 1 < len(stages):
            if st[0] == "mix":
                conv_state["defer"] = True
                conv_stage(stages[si + 1])
                conv_state["defer"] = False
                conv_state["every"] = max(1, int(0.8 * 72 / max(1, len(conv_pending))))
            else:
                conv_stage(stages[si + 1])
        if st[0] == "ffn":
            ffn(st[1], st[2], first=(si == 0))
            pr.barrier()
        elif st[1] % 2 == 0:
            even_mixer(st[1])
        else:
            odd_mixer(st[1])
        conv_flush()

    pr.emit(nc, es)
    es.close()
    nc._prog_stats = (pr.n_sems, pr.stats)
    return nc, NCV, cvo


def prep_shared(cfg, inp):
    DEPTH, NE, NO = cfg.DEPTH, cfg.NE, cfg.NO
    sh = {}
    sh["w1"] = np.concatenate([tile_in(inp["ffn_w1"][l, s]) for l in range(DEPTH) for s in range(2)], axis=1)
    sh["w3"] = np.concatenate([tile_in(inp["ffn_w3"][l, s]) for l in range(DEPTH) for s in range(2)], axis=1)
    sh["w2"] = np.concatenate([tile_w2(inp["ffn_w2"][l, s]) for l in range(DEPTH) for s in range(2)], axis=1)
    sh["ewi"] = np.concatenate([tile_in(inp["ev_w_in"][i]) for i in range(NE)], axis=1)
    sh["ewo"] = np.concatenate([tile_in(inp["ev_w_out"][i]) for i in range(NE)], axis=1)
    if NO:
        sh["owi"] = np.concatenate([tile_in(inp["od_w_in"][i]) for i in range(NO)], axis=1)
        sh["owo"] = np.concatenate([tile_in(inp["od_w_out"][i]) for i in range(NO)], axis=1)
        sh["opw"] = np.concatenate(
            [np.ascontiguousarray(inp["od_pool_w"][i].reshape(4, 2, P, 256).transpose(2, 0, 1, 3)).reshape(P, -1)
             for i in range(NO)], axis=1)
    ident, bias, invc = const_tables()
    cols = [vec_cols(inp["norm_g"][:DEPTH]),
            np.tile(inp["ev_q_gain"][:NE].T, (1, 1)).reshape(P, NE),
            np.tile(inp["ev_k_gain"][:NE].T, (1, 1)).reshape(P, NE),
            vec_cols(inp["ev_conv_w"][:NE])]
    if NO:
        cols += [vec_cols(inp["od_conv_w"][:NO]), vec_cols(inp["od_conv_b"][:NO]),
                 vec_cols(inp["od_ln_g"][:NO]), vec_cols(inp["od_ln_b"][:NO]),
                 vec_cols(inp["od_pool_scale"][:NO])]
    else:
        cols += [np.zeros((P, 31 * 8), np.float32)] + [np.zeros((P, 8), np.float32)] * 4
    cols.append(invc)
    sh["cvec"] = np.ascontiguousarray(np.concatenate(cols, axis=1).astype(np.float32))
    sh["ident"] = ident
    sh["biast"] = bias
    return sh


def run(cfg, inp, trace=False):
    nc, NCV, cvo = build_program(cfg)
    sh = prep_shared(cfg, inp)
    assert sh["cvec"].shape[1] == NCV, (sh["cvec"].shape, NCV)
    x = np.asarray(inp["x"])
    in_maps = []
    for c in range(cfg.NCORES):
        m = dict(sh)
        m["xT"] = np.ascontiguousarray(x[c].reshape(cfg.S, DC, P).transpose(2, 1, 0))
        in_maps.append(m)
    res = run_bass_kernel_spmd(nc, in_maps, core_ids=list(range(cfg.NCORES)), trace=trace)
    outs = [np.ascontiguousarray(r["yT"].transpose(2, 1, 0)).reshape(cfg.S, D) for r in res.results]
    return np.stack(outs, axis=0), res


def kernel(**inputs):
    cfg = Cfg()
    inp = {k: np.asarray(v) for k, v in inputs.items()}
    out, _ = run(cfg, inp)
    return out.astype(np.float32)
```
